# Optimizing a Trainium2 kernel written in Bass

```python
import jax, jax.numpy as jnp
from jax import lax
import numpy as np

D_MODEL = 1024
BATCH = 8
SEQ = 4096
DEPTH = 1

HEAD_DIM = 64
D_MIX = D_MODEL
NSA_WIDTH = D_MIX // 2
NSA_HEADS = NSA_WIDTH // HEAD_DIM
NSA_KV_HEADS = 2
NSA_GQA = NSA_HEADS // NSA_KV_HEADS
KV_WIDTH = NSA_KV_HEADS * HEAD_DIM
CMP_BLOCK = 32
CMP_STRIDE = 16
CMP_HIDDEN = 2 * HEAD_DIM
SEL_BLOCK = 64
SEL_TOPK = 16
WINDOW = 512
Q_BLOCK = 64
N_BRANCH = 3
RWKV_WIDTH = D_MIX - NSA_WIDTH
RWKV_HEADS = RWKV_WIDTH // HEAD_DIM
DECAY_LORA = 64
AAA_LORA = 64
GATE_LORA = 128
D_FF = 4 * D_MODEL
RMS_EPS = 1e-6
GN_EPS = HEAD_DIM * 1e-5
NEG_BIG = -1e30
FORCED_SCORE = 1e6
NSA_SIZES = (NSA_WIDTH,) + (KV_WIDTH,) * 6 + (NSA_HEADS * N_BRANCH,)
RWKV_SIZES = (RWKV_WIDTH,) * 3 + (DECAY_LORA, AAA_LORA, GATE_LORA)
NSA_COLS = sum(NSA_SIZES)
RWKV_COLS = sum(RWKV_SIZES)
D_IN_PROJ = NSA_COLS + RWKV_COLS

kernel_name = 'hybrid_nsa_rwkv7_layer'


def _split_points(sizes):
    return [int(v) for v in np.cumsum(sizes)[:-1]]


def rmsnorm(u, g):
    uf = u.astype(jnp.float32)
    y = uf * lax.rsqrt(jnp.mean(uf * uf, axis=-1, keepdims=True) + RMS_EPS)
    return (y * g.astype(jnp.float32)).astype(u.dtype)


def alibi_slopes(n):
    start = 2.0 ** (-8.0 / n)
    return (start ** np.arange(1, n + 1)).astype(np.float32)


def softmax_masked(s, valid):
    return jax.nn.softmax(jnp.where(valid, s, NEG_BIG), axis=-1) * valid


def compress_blocks(kv, pos, w1, w2):
    B, T = kv.shape[0], kv.shape[1]
    n_cmp = (T - CMP_BLOCK) // CMP_STRIDE + 1
    idx = np.arange(n_cmp)[:, None] * CMP_STRIDE + np.arange(CMP_BLOCK)[None, :]
    blocks = kv[:, idx] + pos[None, None, :, None, :]
    flat = blocks.transpose(0, 1, 3, 2, 4).reshape(B, n_cmp, NSA_KV_HEADS, CMP_BLOCK * HEAD_DIM)
    return jax.nn.gelu(flat @ w1) @ w2


def nsa_mixer(cols, slopes, gate_b, q_g, kc_g, ks_g, kw_g, ck_pos, ck_w1, ck_w2, cv_pos, cv_w1, cv_w2):
    B, T, _ = cols.shape
    H, KVH, G, dh = NSA_HEADS, NSA_KV_HEADS, NSA_GQA, HEAD_DIM
    q, kc, vc, ks, vs, kw, vw, gl = jnp.split(cols, _split_points(NSA_SIZES), axis=-1)
    q = rmsnorm(q.reshape(B, T, H, dh), q_g) * (dh ** -0.5)
    kc = rmsnorm(compress_blocks(kc.reshape(B, T, KVH, dh), ck_pos, ck_w1, ck_w2), kc_g)
    vc = compress_blocks(vc.reshape(B, T, KVH, dh), cv_pos, cv_w1, cv_w2)
    ks = rmsnorm(ks.reshape(B, T, KVH, dh), ks_g)
    vs = vs.reshape(B, T, KVH, dh)
    kw = rmsnorm(kw.reshape(B, T, KVH, dh), kw_g)
    vw = vw.reshape(B, T, KVH, dh)
    gates = jax.nn.sigmoid(gl + gate_b).reshape(B, T, KVH, G, N_BRANCH)

    n_cmp = kc.shape[1]
    n_sel = T // SEL_BLOCK
    top_k = min(SEL_TOPK, n_sel)
    cmp_end = jnp.arange(n_cmp) * CMP_STRIDE + (CMP_BLOCK - 1)
    ci = np.arange(n_cmp)[:, None] * CMP_STRIDE
    sj = np.arange(n_sel)[None, :] * SEL_BLOCK
    overlap = jnp.asarray(((ci <= sj + SEL_BLOCK - 1) & (ci + CMP_BLOCK - 1 >= sj)).astype(np.float32))
    ks_blk = ks.reshape(B, n_sel, SEL_BLOCK, KVH, dh).transpose(0, 3, 1, 2, 4)
    vs_blk = vs.reshape(B, n_sel, SEL_BLOCK, KVH, dh).transpose(0, 3, 1, 2, 4)
    kw_pad = jnp.pad(kw, ((0, 0), (WINDOW, 0), (0, 0), (0, 0)))
    vw_pad = jnp.pad(vw, ((0, 0), (WINDOW, 0), (0, 0), (0, 0)))
    m = slopes.reshape(KVH, G)
    b_idx = jnp.arange(B)[:, None, None, None]
    h_idx = jnp.arange(KVH)[None, :, None, None]
    sel_ids = jnp.arange(n_sel)
    blk_off = jnp.arange(SEL_BLOCK)
    win_off = jnp.arange(WINDOW + Q_BLOCK)

    def query_block(q0):
        t = q0 + jnp.arange(Q_BLOCK)
        qb = lax.dynamic_slice_in_dim(q, q0, Q_BLOCK, axis=1).reshape(B, Q_BLOCK, KVH, G, dh)
        gb = lax.dynamic_slice_in_dim(gates, q0, Q_BLOCK, axis=1)
        d_c = (t[:, None] - cmp_end[None, :]).astype(jnp.float32)
        s_c = jnp.einsum('bqhgd,bnhd->bhgqn', qb, kc).astype(jnp.float32) - m[:, :, None, None] * d_c
        p_c = softmax_masked(s_c, d_c >= 0)
        o_c = jnp.einsum('bhgqn,bnhd->bqhgd', p_c.astype(vc.dtype), vc)
        imp = jnp.einsum('bhgqn,nj->bhqj', p_c, overlap)
        cur = (t // SEL_BLOCK)[:, None]
        forced = (sel_ids == 0) | (sel_ids == cur) | (sel_ids == cur - 1)
        imp = jnp.where(forced, FORCED_SCORE, jnp.where(sel_ids > cur, NEG_BIG, imp))
        _, sel = lax.top_k(imp, top_k)
        k_g = ks_blk[b_idx, h_idx, sel]
        v_g = vs_blk[b_idx, h_idx, sel]
        d_s = (t[None, None, :, None, None] - (sel[..., None] * SEL_BLOCK + blk_off)).astype(jnp.float32)
        s_s = jnp.einsum('bqhgd,bhqksd->bhgqks', qb, k_g).astype(jnp.float32) - m[:, :, None, None, None] * d_s[:, :, None]
        valid_s = (d_s >= 0)[:, :, None].reshape(B, KVH, 1, Q_BLOCK, top_k * SEL_BLOCK)
        p_s = softmax_masked(s_s.reshape(B, KVH, G, Q_BLOCK, top_k * SEL_BLOCK), valid_s)
        p_s = p_s.reshape(B, KVH, G, Q_BLOCK, top_k, SEL_BLOCK)
        o_s = jnp.einsum('bhgqks,bhqksd->bqhgd', p_s.astype(v_g.dtype), v_g)
        kwb = lax.dynamic_slice_in_dim(kw_pad, q0, WINDOW + Q_BLOCK, axis=1)
        vwb = lax.dynamic_slice_in_dim(vw_pad, q0, WINDOW + Q_BLOCK, axis=1)
        s_pos = q0 - WINDOW + win_off
        d_w = t[:, None] - s_pos[None, :]
        valid_w = (d_w >= 0) & (d_w < WINDOW) & (s_pos[None, :] >= 0)
        s_w = jnp.einsum('bqhgd,bshd->bhgqs', qb, kwb).astype(jnp.float32) - m[:, :, None, None] * d_w.astype(jnp.float32)
        p_w = softmax_masked(s_w, valid_w)
        o_w = jnp.einsum('bhgqs,bshd->bqhgd', p_w.astype(vwb.dtype), vwb)
        o = gb[..., 0:1] * o_c + gb[..., 1:2] * o_s + gb[..., 2:3] * o_w
        return o.reshape(B, Q_BLOCK, H * dh)

    out = lax.map(query_block, jnp.arange(T // Q_BLOCK) * Q_BLOCK)
    return out.transpose(1, 0, 2, 3).reshape(B, T, H * dh)


def wkv7_scan(r, w, k, v, a, b):
    B, T, H, N = r.shape

    def step(state, inp):
        r_t, w_t, k_t, v_t, a_t, b_t = inp
        sa = jnp.einsum('bhvk,bhk->bhv', state, a_t)
        state = state * w_t[:, :, None, :] + sa[..., None] * b_t[:, :, None, :] + v_t[..., None] * k_t[:, :, None, :]
        return state, jnp.einsum('bhvk,bhk->bhv', state, r_t)

    xs = tuple(jnp.swapaxes(u, 0, 1) for u in (r, w, k, v, a, b))
    _, ys = lax.scan(step, jnp.zeros((B, H, N, N), jnp.float32), xs)
    return jnp.swapaxes(ys, 0, 1)


def rwkv7_mixer(cols, mu, w0, w2, a0, a2, g2, k_k, k_a, r_k, lnx_w, lnx_b):
    B, T, _ = cols.shape
    H, N = RWKV_HEADS, HEAD_DIM
    prev = jnp.pad(cols, ((0, 0), (1, 0), (0, 0)))[:, :-1]
    z = cols + (prev - cols) * mu
    r, k, v, xw, xa, xg = jnp.split(z, _split_points(RWKV_SIZES), axis=-1)
    w_log = -jax.nn.softplus(-(w0 + jnp.tanh(xw) @ w2)) - 0.5
    decay = jnp.exp(-jnp.exp(w_log.astype(jnp.float32)))
    a = jax.nn.sigmoid(a0 + xa @ a2)
    g = jax.nn.sigmoid(xg) @ g2
    heads = lambda u: u.reshape(B, T, H, N).astype(jnp.float32)
    kk = heads(k * k_k)
    kk = kk * lax.rsqrt(jnp.maximum(jnp.sum(kk * kk, axis=-1, keepdims=True), 1e-24))
    k = k * (1 + (a - 1) * k_a)
    rh, kh, vh, ah = heads(r), heads(k), heads(v), heads(a)
    y = wkv7_scan(rh, decay.reshape(B, T, H, N), kh, vh, -kk, kk * ah)
    mean = jnp.mean(y, axis=-1, keepdims=True)
    var = jnp.mean(jnp.square(y - mean), axis=-1, keepdims=True)
    y = ((y - mean) * lax.rsqrt(var + GN_EPS)).reshape(B, T, RWKV_WIDTH) * lnx_w + lnx_b
    bonus = jnp.sum(rh * kh * r_k, axis=-1, keepdims=True) * vh
    y = y + bonus.reshape(B, T, RWKV_WIDTH)
    return (y * g).astype(cols.dtype)


def setup_inputs(seed: int = 0) -> dict:
    key = jax.random.key(seed)
    ks = jax.random.split(key, 32)
    L, dh = DEPTH, HEAD_DIM
    nrm = lambda k, shape, scale: jax.random.normal(k, shape, jnp.float32) * scale
    gain = lambda k, shape: 1.0 + 0.02 * jax.random.normal(k, shape, jnp.float32)
    return {
        'x': jax.random.normal(ks[0], (BATCH, SEQ, D_MODEL), jnp.float32),
        'ln_mix_g': gain(ks[1], (L, D_MODEL)),
        'w_in': nrm(ks[2], (L, D_MODEL, D_IN_PROJ), D_MODEL ** -0.5),
        'nsa_gate_b': nrm(ks[3], (L, NSA_HEADS * N_BRANCH), 0.1),
        'q_norm_g': gain(ks[4], (L, dh)),
        'kc_norm_g': gain(ks[5], (L, dh)),
        'ks_norm_g': gain(ks[6], (L, dh)),
        'kw_norm_g': gain(ks[7], (L, dh)),
        'cmp_k_pos': nrm(ks[8], (L, CMP_BLOCK, dh), 0.1),
        'cmp_k_w1': nrm(ks[9], (L, CMP_BLOCK * dh, CMP_HIDDEN), (CMP_BLOCK * dh) ** -0.5),
        'cmp_k_w2': nrm(ks[10], (L, CMP_HIDDEN, dh), CMP_HIDDEN ** -0.5),
        'cmp_v_pos': nrm(ks[11], (L, CMP_BLOCK, dh), 0.1),
        'cmp_v_w1': nrm(ks[12], (L, CMP_BLOCK * dh, CMP_HIDDEN), (CMP_BLOCK * dh) ** -0.5),
        'cmp_v_w2': nrm(ks[13], (L, CMP_HIDDEN, dh), CMP_HIDDEN ** -0.5),
        'rwkv_mu': jax.random.uniform(ks[14], (L, RWKV_COLS), jnp.float32),
        'rwkv_w0': jax.random.uniform(ks[15], (L, RWKV_WIDTH), jnp.float32, -4.0, 1.0),
        'rwkv_w2': nrm(ks[16], (L, DECAY_LORA, RWKV_WIDTH), 0.1),
        'rwkv_a0': nrm(ks[17], (L, RWKV_WIDTH), 0.1),
        'rwkv_a2': nrm(ks[18], (L, AAA_LORA, RWKV_WIDTH), 0.1),
        'rwkv_g2': nrm(ks[19], (L, GATE_LORA, RWKV_WIDTH), GATE_LORA ** -0.5),
        'rwkv_k_k': 0.85 + 0.02 * jax.random.normal(ks[20], (L, RWKV_WIDTH), jnp.float32),
        'rwkv_k_a': gain(ks[21], (L, RWKV_WIDTH)),
        'rwkv_r_k': nrm(ks[22], (L, RWKV_HEADS, dh), 0.1),
        'rwkv_lnx_w': gain(ks[23], (L, RWKV_WIDTH)),
        'rwkv_lnx_b': nrm(ks[24], (L, RWKV_WIDTH), 0.02),
        'w_out': nrm(ks[25], (L, D_MIX, D_MODEL), D_MIX ** -0.5),
        'ln_ffn_g': gain(ks[26], (L, D_MODEL)),
        'w_ff1': nrm(ks[27], (L, D_MODEL, D_FF), D_MODEL ** -0.5),
        'w_ff2': nrm(ks[28], (L, D_FF, D_MODEL), D_FF ** -0.5),
    }


def reference(x, ln_mix_g, w_in, nsa_gate_b, q_norm_g, kc_norm_g, ks_norm_g, kw_norm_g,
              cmp_k_pos, cmp_k_w1, cmp_k_w2, cmp_v_pos, cmp_v_w1, cmp_v_w2,
              rwkv_mu, rwkv_w0, rwkv_w2, rwkv_a0, rwkv_a2, rwkv_g2, rwkv_k_k, rwkv_k_a, rwkv_r_k,
              rwkv_lnx_w, rwkv_lnx_b, w_out, ln_ffn_g, w_ff1, w_ff2):
    slopes = jnp.asarray(alibi_slopes(NSA_HEADS))
    for l in range(DEPTH):
        proj = rmsnorm(x, ln_mix_g[l]) @ w_in[l]
        y_nsa = nsa_mixer(proj[..., :NSA_COLS], slopes, nsa_gate_b[l], q_norm_g[l], kc_norm_g[l],
                          ks_norm_g[l], kw_norm_g[l], cmp_k_pos[l], cmp_k_w1[l], cmp_k_w2[l],
                          cmp_v_pos[l], cmp_v_w1[l], cmp_v_w2[l])
        y_rwkv = rwkv7_mixer(proj[..., NSA_COLS:], rwkv_mu[l], rwkv_w0[l], rwkv_w2[l], rwkv_a0[l],
                             rwkv_a2[l], rwkv_g2[l], rwkv_k_k[l], rwkv_k_a[l], rwkv_r_k[l],
                             rwkv_lnx_w[l], rwkv_lnx_b[l])
        x = x + jnp.concatenate([y_nsa, y_rwkv], axis=-1) @ w_out[l]
        hidden = jnp.square(jax.nn.relu(rmsnorm(x, ln_ffn_g[l]) @ w_ff1[l]))
        x = x + hidden @ w_ff2[l]
    return x
```

```python
import numpy as np
import concourse.bass as bass
import concourse.mybir as mybir
from concourse.bass_utils import run_bass_kernel_spmd

F32 = mybir.dt.float32
BF16 = mybir.dt.bfloat16
AF = mybir.ActivationFunctionType
ALU = mybir.AluOpType

ENGS = ["pe", "act", "dve", "pool", "sp"]
T = 4096
NT = 8
DEC = 0.6065306597126334


class Prog:
    NSLOT = 12

    def __init__(self):
        self.nc = bass.Bass("TRN2", target_bir_lowering=False)
        self.ops = {e: [] for e in ENGS}
        self.count = {e: 0 for e in ENGS}
        self.last_w = {}
        self.readers = {}
        self.waited = {e: {} for e in ENGS}
        self.slot_next = {e: 0 for e in ENGS}
        self.slot_gen = {}
        self.sb_off = 16512
        self.sb_limit = 229312
        self.sb_marks = []
        self.n_t = 0
        self.all_tokens = {}

    def sb(self, shape, dtype, name=None):
        esz = {F32: 4, BF16: 2}[dtype]
        nbytes = int(np.prod(shape[1:])) * esz
        off = (self.sb_off + 63) // 64 * 64
        self.n_t += 1
        t = self.nc.alloc_sbuf_tensor_at(name or f"t{self.n_t}", list(shape), dtype, offset=off)
        self.sb_off = off + nbytes
        self.sb_peak = max(getattr(self, "sb_peak", 0), self.sb_off)
        assert self.sb_off <= self.sb_limit, f"SBUF overflow {self.sb_off} > {self.sb_limit}"
        return t

    def sb_at(self, shape, dtype, off, name=None):
        self.n_t += 1
        return self.nc.alloc_sbuf_tensor_at(name or f"t{self.n_t}", list(shape), dtype, offset=off)

    def mark(self):
        self.sb_marks.append(self.sb_off)

    def release(self):
        self.sb_off = self.sb_marks.pop()

    def _deps(self, eng, reads, writes):
        toks = {}

        def add(tok):
            if tok is None:
                return
            s, v = tok
            if toks.get(s, 0) < v:
                toks[s] = v

        for r in reads:
            add(self.last_w.get(r))
        for w in writes:
            add(self.last_w.get(w))
            for s, v in self.readers.get(w, {}).items():
                add((s, v))
        out = []
        for s, v in toks.items():
            if s == ("c", "pe") and eng == "pe":
                continue
            if self.waited[eng].get(s, 0) >= v:
                continue
            self.waited[eng][s] = v
            out.append((s, v))
        return out

    def _commit(self, tok, reads, writes):
        s, v = tok
        self.all_tokens[s] = max(self.all_tokens.get(s, 0), v)
        for r in reads:
            d = self.readers.setdefault(r, {})
            if d.get(s, 0) < v:
                d[s] = v
        for w in writes:
            self.last_w[w] = tok
            self.readers[w] = {}

    @staticmethod
    def _is_psum(k):
        return isinstance(k, str) and len(k) == 2 and k[0] == "B" and k[1].isdigit()

    def defer_begin(self):
        self._defer = []

    def defer_end(self):
        lst, self._defer = self._defer, None
        return lst

    def replay(self, lst, n):
        for _ in range(min(n, len(lst))):
            kind, args = lst.pop(0)
            (self.op if kind == "op" else self.dma)(*args)

    def op(self, eng, fn, reads=(), writes=()):
        if getattr(self, "_defer", None) is not None:
            self._defer.append(("op", (eng, fn, list(reads), list(writes))))
            return
        writes = list(writes) + [k for k in reads if self._is_psum(k) and k not in writes]
        waits = self._deps(eng, reads, writes)
        self.count[eng] += 1
        tok = (("c", eng), self.count[eng])
        self.ops[eng].append((waits, fn, tok))
        self._commit(tok, reads, writes)

    def dma(self, eng, out, in_, reads=(), writes=()):
        if getattr(self, "_defer", None) is not None:
            self._defer.append(("dma", (eng, out, in_, list(reads), list(writes))))
            return
        i = self.slot_next[eng]
        self.slot_next[eng] = (i + 1) % self.NSLOT
        s = ("d", eng, i)
        gen = self.slot_gen.get(s, 0)
        waits = self._deps(eng, reads, writes)
        if gen > 0 and self.waited[eng].get(s, 0) < 16 * gen:
            self.waited[eng][s] = 16 * gen
            waits.append((s, 16 * gen))
        self.slot_gen[s] = gen + 1
        tok = (s, 16 * (gen + 1))
        self.ops[eng].append((waits, lambda e: e.dma_start(out=out, in_=in_), tok))
        self._commit(tok, reads, writes)

    def phase(self, label):
        if not hasattr(self, 'marks'):
            self.marks = []
        self.marks.append((label, dict(self.count)))

    def barrier(self):
        for e in ENGS:
            waits = []
            for s, v in self.all_tokens.items():
                if s == ("c", e):
                    continue
                if self.waited[e].get(s, 0) >= v:
                    continue
                self.waited[e][s] = v
                waits.append((s, v))
            if waits:
                self.ops[e].append((waits, None, None))

    def emit(self):
        nc = self.nc
        from contextlib import ExitStack

        with ExitStack() as es:
            sems = {}
            for e in ENGS:
                sems[("c", e)] = es.enter_context(nc.semaphore(f"c_{e}"))
            for s in self.slot_gen:
                sems[s] = es.enter_context(nc.semaphore(f"d_{s[1]}_{s[2]}"))
            self.barrier()
            block = es.enter_context(nc.Block())

            def run(e, engine):
                for waits, fn, tok in self.ops[e]:
                    for s, v in waits:
                        engine.wait_ge(sems[s], v)
                    if fn is None:
                        continue
                    ins = fn(engine)
                    ins.then_inc(sems[tok[0]], 16 if tok[0][0] == "d" else 1)

            @block.tensor
            def _(eng):
                run("pe", eng)

            @block.scalar
            def _(eng):
                run("act", eng)

            @block.vector
            def _(eng):
                run("dve", eng)

            @block.gpsimd
            def _(eng):
                run("pool", eng)

            @block.sync
            def _(eng):
                run("sp", eng)

        return nc


def MM(out, lhsT, rhs, start=True, stop=True):
    return lambda e: e.matmul(out, lhsT=lhsT, rhs=rhs, start=start, stop=stop)


def TR(out, in_, ident):
    return lambda e: e.transpose(out, in_, ident)


def ACT(out, in_, func, **kw):
    return lambda e: e.activation(out=out, in_=in_, func=func, **kw)


def TT(out, a, b, op):
    return lambda e: e.tensor_tensor(out=out, in0=a, in1=b, op=op)


def TS(out, a, s1, op0, s2=None, op1=None):
    if op1 is None:
        return lambda e: e.tensor_scalar(out=out, in0=a, scalar1=s1, scalar2=None, op0=op0)
    return lambda e: e.tensor_scalar(out=out, in0=a, scalar1=s1, scalar2=s2, op0=op0, op1=op1)


def STT(out, a, s, b, op0, op1):
    return lambda e: e.scalar_tensor_tensor(out=out, in0=a, scalar=s, in1=b, op0=op0, op1=op1)


def CP(out, in_):
    return lambda e: e.tensor_copy(out=out, in_=in_)


def MS(out, v):
    return lambda e: e.memset(out, v)


def RCP(out, in_):
    return lambda e: e.reciprocal(out=out, in_=in_)


def SCAN(out, d0, d1):
    return lambda e: e.tensor_tensor_scan(out=out, data0=d0, data1=d1, initial=0.0, op0=ALU.mult, op1=ALU.add)


PV_G = 0
PV_MU = 8
PV_W0 = 22
PV_A0 = 26
PV_KK = 30
PV_KA = 34
PV_RK = 38
PV_LW = 42
PV_LB = 46
NPV = 64

RW0 = 1536
WCOLS = RW0 + 1792
PV_QG = 50
PV_KSG = 51
PV_KWG = 52
PV_KCG = 53
PV_G2 = 54
SLOPES = [2.0 ** -(i + 1) for i in range(8)]
NEGB = -1.0e9


def build(stage="all", n_hp=4, n_tt=NT, n_att=NT):
    P = Prog()
    nc = P.nc
    xT = nc.dram_tensor("xT", [8, 128, T], F32, kind="ExternalInput").ap()
    w_in = nc.dram_tensor("w_in", [8, 128, WCOLS], F32, kind="ExternalInput").ap()
    pvec = nc.dram_tensor("pvec", [128, NPV], F32, kind="ExternalInput").ap()
    w2a2 = nc.dram_tensor("w2a2", [128, 512], F32, kind="ExternalInput").ap()
    g2d = nc.dram_tensor("g2", [128, 512], F32, kind="ExternalInput").ap()
    cst = nc.dram_tensor("cst", [128, 7 * 128], F32, kind="ExternalInput").ap()
    mixT = nc.dram_tensor("mixT", [8, 128, T], BF16, kind=("ExternalOutput" if stage != "all" else "Internal")).ap()

    B = [nc.alloc_psum_tensor(f"bank{i}", [128, 512], F32) for i in range(6)]
    B6 = nc.alloc_psum_tensor("bank6", [128, 1024], BF16)
    B7 = nc.alloc_psum_tensor("bank7", [128, 512], F32)

    pv = P.sb([128, NPV], F32)
    cf = P.sb([128, 7 * 128], F32)
    identb = P.sb([128, 128], BF16)
    BOb = P.sb([128, 128], BF16)
    onesb = P.sb([128, 128], BF16)
    ones32 = P.sb([128, 128], F32)
    omka = P.sb([128, 4], F32)
    w2a2b = P.sb([128, 512], BF16)
    g2b = P.sb([128, 512], BF16)
    P.dma("sp", pv[:], pvec, writes=["pv"])
    P.dma("sp", cf[:], cst, writes=["cf"])
    P.op("dve", CP(identb[:], cf[:, 640:768]), reads=["cf"], writes=["identb"])
    P.op("dve", CP(BOb[:], cf[:, 768:896]), reads=["cf"], writes=["BOb"])
    P.op("pool", MS(ones32[:], 1.0), writes=["ones32"])
    P.op("pool", MS(onesb[:], 1.0), writes=["onesb"])
    P.op("dve", TS(omka[:], pv[:, PV_KA:PV_KA + 4], -1.0, ALU.mult, 1.0, ALU.add), reads=["pv"], writes=["omka"])
    MASK4 = cf[:, 0:512]
    MIU = cf[:, 512:640]

    XN_OFF = 229312 - 8 * T * 2
    xn = P.sb_at([128, 8, T], BF16, XN_OFF)
    P.sb_limit = XN_OFF

    P.mark()
    stg = [P.sb([128, 512], F32) for _ in range(2)]
    P.dma("act", stg[0][:], w2a2, writes=["stg0"])
    P.dma("act", stg[1][:], g2d, writes=["stg1"])
    P.op("pool", CP(w2a2b[:], stg[0][:]), reads=["stg0"], writes=["w2a2b"])
    P.op("pool", CP(g2b[:], stg[1][:]), reads=["stg1"], writes=["g2b"])
    xin = [P.sb([128, 8, 512], F32) for _ in range(2)]
    sq = P.sb([128, 8, 512], BF16)
    sd = P.sb([128, 512], F32)
    rstd = P.sb([128, 512], F32)
    for tt in range(NT):
        ts = slice(tt * 512, (tt + 1) * 512)
        xi = xin[tt % 2]
        kx = f"xin{tt % 2}"
        P.dma("sp" if tt % 2 == 0 else "act", xi[:], xT[:, :, ts].rearrange("c p t -> p c t"), writes=[kx])
        P.op("act", ACT(sq[:], xi[:], AF.Square), reads=[kx], writes=["sq"])
        for dc in range(8):
            P.op("pe", MM(B[0][:], onesb[:], sq[:, dc, :], start=dc == 0, stop=dc == 7),
                 reads=["sq", "onesb"], writes=["B0"])
        P.op("act", ACT(sd[:], B[0][:], AF.Ln, scale=1.0 / 1024, bias=1e-6), reads=["B0"], writes=["sd"])
        P.op("act", ACT(rstd[:], sd[:], AF.Exp, scale=-0.5), reads=["sd"], writes=["rstd"])
        for dc in range(8):
            P.op("dve", STT(xn[:, dc, ts], xi[:, dc, :], pv[:, PV_G + dc:PV_G + dc + 1], rstd[:], ALU.mult, ALU.mult),
                 reads=[kx, "pv", "rstd"], writes=["xn"])
    P.barrier()
    P.release()
    if stage == "1a":
        dbg = nc.dram_tensor("dbg", [128, 8, T], BF16, kind="ExternalOutput").ap()
        P.dma("sp", dbg, xn[:], reads=["xn"])
        return P

    P.phase('1B')
    P.mark()
    Wr = P.sb([128, 8, 1792], BF16)
    P.mark()
    wst = [P.sb([128, 1792], F32) for _ in range(2)]
    for dc in range(8):
        k = f"wst{dc % 2}"
        P.dma("sp" if dc % 2 == 0 else "act", wst[dc % 2][:], w_in[dc, :, RW0:RW0 + 1792], writes=[k])
        P.op("pool" if dc % 2 == 0 else "dve", CP(Wr[:, dc, :], wst[dc % 2][:]), reads=[k], writes=["Wr"])
    P.barrier()
    P.release()

    _rb = (P.sb_off + 63) // 64 * 64
    cbs = [P.sb_at([128, 513], F32, _rb + i_ * 2112) for i_ in range(2)]
    S1r = [[P.sb_at([128, 512], BF16, _rb + (2 * g_ + h_) * 1024) for h_ in range(2)] for g_ in range(2)]
    ARKr = [[P.sb_at([128, 128], BF16, _rb + 4096 + (2 * g_ + h_) * 256) for h_ in range(2)] for g_ in range(2)]
    P.sb_off = _rb + 5120
    P.sb_peak = max(P.sb_peak, P.sb_off)
    cb_ctr = [0]
    ip_ctr = [0]
    tnames = ["sg", "al", "cs", "dm", "E1", "E2", "E3", "kk", "kp", "t1"]
    tm_alias = {"ka": "sg", "t2": "t1"}
    tm = {n: P.sb([128, 512], F32) for n in tnames}
    for a_, b_ in tm_alias.items():
        tm[a_] = tm[b_]
    kk2 = P.sb([128, 512], BF16)
    dtmp = tm["t1"]

    def inproj(wt, wkey, wcol, dst, dkey, mu_col, post=None):
        for tt in range(NT):
            ts = slice(tt * 512, (tt + 1) * 512)
            bi = ip_ctr[0] % 2
            ip_ctr[0] += 1
            bank, bk = B[bi], f"B{bi}"
            for dc in range(8):
                P.op("pe", MM(bank[:], wt[:, dc, wcol:wcol + 128], xn[:, dc, ts], start=dc == 0, stop=dc == 7),
                     reads=[wkey, "xn"], writes=[bk])
            if mu_col is None:
                P.op("act", ACT(dst[:, ts], bank[:], AF.Copy), reads=[bk], writes=[dkey])
            else:
                i = cb_ctr[0]
                cb_ctr[0] += 1
                cur, prev = cbs[i % 2], cbs[(i - 1) % 2]
                kc, kp = f"cb{i % 2}", f"cb{(i - 1) % 2}"
                if tt == 0:
                    P.op("pool", MS(cur[:, 0:1], 0.0), writes=[kc])
                else:
                    P.op("pool", CP(cur[:, 0:1], prev[:, 512:513]), reads=[kp], writes=[kc])
                P.op("act", ACT(cur[:, 1:513], bank[:], AF.Copy), reads=[bk], writes=[kc])
                P.op("dve", TT(dtmp[:], cur[:, 0:512], cur[:, 1:513], ALU.subtract), reads=[kc], writes=["t1"])
                P.op("dve", STT(dst[:, ts], dtmp[:], pv[:, mu_col:mu_col + 1], cur[:, 1:513], ALU.mult, ALU.add),
                     reads=["t1", kc, "pv"], writes=[dkey])
            if post is not None:
                post(tt, ts)

    TWXA = P.sb([128, T], BF16)
    SG = P.sb([128, T], BF16)
    inproj(Wr, "Wr", 1536, TWXA, "TWXA", PV_MU + 12,
           post=lambda tt, ts: P.op("act", ACT(TWXA[0:64, ts], TWXA[0:64, ts], AF.Tanh), reads=["TWXA"], writes=["TWXA"]))
    inproj(Wr, "Wr", 1664, SG, "SG", PV_MU + 13,
           post=lambda tt, ts: P.op("act", ACT(SG[:, ts], SG[:, ts], AF.Sigmoid), reads=["SG"], writes=["SG"]))

    if stage == "1b_tw":
        dbg = nc.dram_tensor("dbg", [2, 128, T], BF16, kind="ExternalOutput").ap()
        P.dma("sp", dbg[0], TWXA[:], reads=["TWXA"])
        P.dma("sp", dbg[1], SG[:], reads=["SG"])
        return P
    Zr = P.sb([128, T], BF16)
    Zk = P.sb([128, T], BF16)
    Zv = P.sb([128, T], BF16)
    rkr = P.sb([128, 512], BF16)
    ncC = P.sb([128, 4], F32)
    gC = P.sb([128, 32], F32)
    opn = ["At", "Bt", "Kt", "Rt", "BGt", "KGt"]
    ops_ = [{n: P.sb([128, 512], BF16) for n in opn} for _ in range(2)]
    BON = [P.sb([128, 512], F32) for _ in range(2)]
    YT = [P.sb([128, 512], F32) for _ in range(2)]
    S1 = [[P.sb([128, 512], BF16) for _ in range(2)] for _ in range(2)]
    ARK = [[P.sb([128, 128], BF16) for _ in range(2)] for _ in range(2)]
    TOK = [P.sb([128, 512], BF16) for _ in range(2)]
    X32 = P.sb([128, 2, 256], F32)
    Xbf = P.sb([128, 2, 256], BF16)
    LTn = [[P.sb([128, 512], BF16) for _ in range(2)] for _ in range(2)]
    WTs = P.sb([128, 2, 128], BF16)
    Ubf = [P.sb([128, 128], BF16) for _ in range(2)]
    Upad = [[P.sb([128, 128], BF16) for _ in range(2)] for _ in range(2)]
    Vpad = [[P.sb([128, 128], BF16) for _ in range(2)] for _ in range(2)]
    for a_ in range(2):
        for b_ in range(2):
            P.op("pool", MS(Upad[a_][b_][:], 0.0), writes=[f"Upad{a_}"])
            P.op("pool", MS(Vpad[a_][b_][:], 0.0), writes=[f"Vpad{a_}"])
    M32 = P.sb([128, 128], F32)
    Mb2 = [P.sb([128, 128], BF16) for _ in range(2)]
    ybf = kk2
    ym = tm["E3"]
    MO = [P.sb([128, 512], BF16) for _ in range(1)]

    def prep(hp, tt, s):
        ts = slice(tt * 512, (tt + 1) * 512)
        hc = slice(hp * 128, (hp + 1) * 128)
        O = ops_[s]
        ok = lambda n: f"{n}{s}"
        col = lambda base: pv[:, base + hp:base + hp + 1]
        P.op("pe", MM(B[0][:], w2a2b[0:64, hc], TWXA[0:64, ts]), reads=["w2a2b", "TWXA"], writes=["B0"])
        P.op("pe", MM(B[1][:], w2a2b[64:128, hc], TWXA[64:128, ts]), reads=["w2a2b", "TWXA"], writes=["B1"])
        P.op("act", ACT(tm["sg"][:], B[0][:], AF.Sigmoid, bias=col(PV_W0)), reads=["B0", "pv"], writes=["sg"])
        P.op("act", ACT(tm["al"][:], B[1][:], AF.Sigmoid, bias=col(PV_A0)), reads=["B1", "pv"], writes=["al"])
        for c in range(4):
            c_ = slice(c * 128, (c + 1) * 128)
            P.op("dve", SCAN(tm["cs"][:, c_], ones32[:], tm["sg"][:, c_]), reads=["sg", "ones32"], writes=["cs"])
        P.op("dve", TS(ncC[:], tm["cs"][:, 127::128], -DEC, ALU.mult), reads=["cs"], writes=["ncC"])
        P.op("act", ACT(gC[:, tt * 4:(tt + 1) * 4], ncC[:], AF.Exp), reads=["ncC"], writes=["gC"])
        P.op("act", ACT(tm["E1"][:], tm["cs"][:], AF.Exp, scale=-DEC), reads=["cs"], writes=["E1"])
        P.op("act", ACT(tm["E2"][:], tm["cs"][:], AF.Exp, scale=DEC), reads=["cs"], writes=["E2"])
        P.op("dve", TT(tm["dm"][:], tm["cs"][:], tm["sg"][:], ALU.subtract), reads=["cs", "sg"], writes=["dm"])
        P.op("act", ACT(tm["dm"][:], tm["dm"][:], AF.Exp, scale=-DEC), reads=["dm"], writes=["dm"])
        for c in range(4):
            c_ = slice(c * 128, (c + 1) * 128)
            P.op("act", ACT(tm["E3"][:, c_], tm["cs"][:, c_], AF.Exp, scale=DEC, bias=ncC[:, c:c + 1]),
                 reads=["cs", "ncC"], writes=["E3"])
        P.op("act", ACT(tm["kk"][:], Zk[:, ts], AF.Copy, scale=col(PV_KK)), reads=["Zk", "pv"], writes=["kk"])
        P.op("act", ACT(kk2[:], tm["kk"][:], AF.Square), reads=["kk"], writes=["kk2"])
        P.op("pe", MM(B[0][:], BOb[:], kk2[:]), reads=["BOb", "kk2"], writes=["B0"])
        P.op("dve", TS(tm["t1"][:], B[0][:], 1e-18, ALU.max), reads=["B0"], writes=["t1"])
        P.op("act", ACT(tm["t1"][:], tm["t1"][:], AF.Ln), reads=["t1"], writes=["t1"])
        P.op("act", ACT(tm["t1"][:], tm["t1"][:], AF.Exp, scale=-0.5), reads=["t1"], writes=["t1"])
        P.op("dve", TT(tm["kk"][:], tm["kk"][:], tm["t1"][:], ALU.mult), reads=["kk", "t1"], writes=["kk"])
        P.op("act", ACT(tm["t2"][:], tm["al"][:], AF.Identity, scale=col(PV_KA), bias=omka[:, hp:hp + 1]),
             reads=["al", "pv", "omka"], writes=["t1"])
        P.op("pool", TT(tm["kp"][:], Zk[:, ts], tm["t2"][:], ALU.mult), reads=["Zk", "t1"], writes=["kp"])
        P.op("dve", STT(rkr[:], tm["kp"][:], col(PV_RK), Zr[:, ts], ALU.mult, ALU.mult), reads=["kp", "pv", "Zr"], writes=["rkr"])
        P.op("pe", MM(B[1][:], BOb[:], rkr[:]), reads=["BOb", "rkr"], writes=["B1"])
        P.op("dve", TT(BON[s][:], B[1][:], Zv[:, ts], ALU.mult), reads=["B1", "Zv"], writes=[ok("BON")])
        P.op("dve", STT(O["At"][:], tm["kk"][:], -1.0, tm["dm"][:], ALU.mult, ALU.mult), reads=["kk", "dm"], writes=[ok("At")])
        P.op("pool", TT(tm["ka"][:], tm["kk"][:], tm["al"][:], ALU.mult), reads=["kk", "al"], writes=["sg"])
        P.op("dve", TT(O["Bt"][:], tm["ka"][:], tm["E2"][:], ALU.mult), reads=["sg", "E2"], writes=[ok("Bt")])
        P.op("pool", TT(O["BGt"][:], tm["ka"][:], tm["E3"][:], ALU.mult), reads=["sg", "E3"], writes=[ok("BGt")])
        P.op("pool", TT(O["Kt"][:], tm["kp"][:], tm["E2"][:], ALU.mult), reads=["kp", "E2"], writes=[ok("Kt")])
        P.op("pool", TT(O["KGt"][:], tm["kp"][:], tm["E3"][:], ALU.mult), reads=["kp", "E3"], writes=[ok("KGt")])
        P.op("dve", TT(O["Rt"][:], Zr[:, ts], tm["E1"][:], ALU.mult), reads=["Zr", "E1"], writes=[ok("Rt")])

    def group(hp, tt, gi, s):
        O = ops_[s]
        ok = lambda n: f"{n}{s}"
        opk = [ok(n) for n in opn]
        G = 2
        ccs = [2 * gi + g for g in range(G)]
        c_s = [slice(cc * 128, (cc + 1) * 128) for cc in ccs]
        gcs = [slice(tt * 512 + cc * 128, tt * 512 + (cc + 1) * 128) for cc in ccs]
        for g in range(G):
            c_ = c_s[g]
            for h in range(2):
                hs = slice(64 * h, 64 * h + 64)
                bank, bk = B[2 + 2 * g + h], f"B{2 + 2 * g + h}"
                P.op("pe", MM(bank[:, 0:128], O["At"][hs, c_], O["Bt"][hs, c_]), reads=opk, writes=[bk])
                P.op("pe", MM(bank[:, 128:256], O["Bt"][hs, c_], O["At"][hs, c_]), reads=opk, writes=[bk])
                P.op("pe", MM(bank[:, 256:384], O["Kt"][hs, c_], O["At"][hs, c_]), reads=opk, writes=[bk])
                P.op("pe", MM(bank[:, 384:512], O["Bt"][hs, c_], O["Rt"][hs, c_]), reads=opk, writes=[bk])
                P.op("dve", TT(S1[g][h][:], bank[:], MASK4, ALU.mult), reads=[bk, "cf"], writes=[f"S1_{g}{h}"])
        if stage == 'g1':
            return True
        for g in range(G):
            c_ = c_s[g]
            o6 = g * 512
            P.op("pe", TR(B6[:, o6:o6 + 128], O["At"][:, c_], identb[:]), reads=opk + ["identb"], writes=["B6"])
            P.op("pe", TR(B6[:, o6 + 128:o6 + 256], Zv[:, gcs[g]], identb[:]), reads=["Zv", "identb"], writes=["B6"])
            P.op("pe", TR(B6[:, o6 + 256:o6 + 384], O["BGt"][:, c_], identb[:]), reads=opk + ["identb"], writes=["B6"])
            P.op("pe", TR(B6[:, o6 + 384:o6 + 512], O["KGt"][:, c_], identb[:]), reads=opk + ["identb"], writes=["B6"])
        for g in range(G):
            P.op("act", ACT(TOK[g][:], B6[:, g * 512:(g + 1) * 512], AF.Copy), reads=["B6"], writes=[f"TOK{g}"])
            for h in range(2):
                P.op("pool", CP(Vpad[g][h][:, 64 * h:64 * h + 64], TOK[g][:, 128 + 64 * h:128 + 64 * h + 64]), reads=[f"TOK{g}"], writes=[f"Vpad{g}"])
        if stage == 'g2':
            return True
        for g in range(G):
            for h in range(2):
                v_ = slice(h * 64, (h + 1) * 64)
                P.op("pe", MM(B7[:, g * 128 + h * 64:g * 128 + (h + 1) * 64], S1[g][h][:, 256:384], TOK[g][:, 128 + h * 64:128 + (h + 1) * 64]),
                     reads=[f"S1_{g}{h}", f"TOK{g}"], writes=["B7"])
        for g in range(G):
            P.op("act", ACT(X32[:, g, 0:128], TOK[g][:, 0:128], AF.Copy), reads=[f"TOK{g}"], writes=[f"X32{g}"])
            P.op("act", ACT(Xbf[:, g, 0:128], TOK[g][:, 0:128], AF.Copy), reads=[f"TOK{g}"], writes=[f"Xbf{g}"])
        b7v = B7[:, 0:256].rearrange("p (g c) -> p g c", c=128)
        P.op("act", ACT(X32[:, :, 128:256], b7v, AF.Copy), reads=["B7"], writes=["X320", "X321"])
        P.op("dve", CP(Xbf[:, :, 128:256], b7v), reads=["B7"], writes=["Xbf0", "Xbf1"])
        if stage == 'g3':
            return True
        X32f = X32[:, :, :].rearrange("p g c -> p (g c)")
        Xbff = Xbf[:, :, :].rearrange("p g c -> p (g c)")
        for j in range(7):
            Ls, Ts, lks = [], [], []
            for g in range(G):
                if j == 0:
                    Ls.append([S1[g][h][:, 0:128] for h in range(2)])
                    Ts.append([S1[g][h][:, 128:256] for h in range(2)])
                    lks.append([f"S1_{g}0", f"S1_{g}1"])
                else:
                    lt = LTn[g][j % 2]
                    Ls.append([lt[:, h * 256:h * 256 + 128] for h in range(2)])
                    Ts.append([lt[:, h * 256 + 128:h * 256 + 256] for h in range(2)])
                    lks.append([f"LTn{g}{j % 2}"])
            for g in range(G):
                if j < 6:
                    bank, bk = B[2 + g], f"B{2 + g}"
                    for h in range(2):
                        P.op("pe", MM(bank[:, h * 256:h * 256 + 128], Ts[g][h], Ls[g][h]), reads=lks[g], writes=[bk])
                        P.op("pe", MM(bank[:, h * 256 + 128:h * 256 + 256], Ls[g][h], Ts[g][h]), reads=lks[g], writes=[bk])
                    P.op("act", ACT(LTn[g][(j + 1) % 2][:], bank[:], AF.Copy), reads=[bk], writes=[f"LTn{g}{(j + 1) % 2}"])
                ab, abk = (B[4], "B4") if g == 0 else (B[5], "B5")
                for h in range(2):
                    for part in range(2):
                        q0 = part * 128 + h * 64
                        P.op("pe", MM(ab[:, q0:q0 + 64], Ts[g][h], Xbf[:, g, q0:q0 + 64]), reads=lks[g] + [f"Xbf{g}"], writes=[abk])
                if j < 6:
                    P.op("dve", TT(Xbf[:, g, :], ab[:, 0:256], X32[:, g, :], ALU.add), reads=[abk, f"X32{g}"], writes=[f"Xbf{g}"])
                else:
                    P.op("dve", TT(Xbf[:, g, 0:128], ab[:, 0:128], X32[:, g, 0:128], ALU.add), reads=[abk, f"X32{g}"], writes=[f"Xbf{g}"])
                P.op("dve", TT(X32[:, g, :], ab[:, 0:256], X32[:, g, :], ALU.add), reads=[abk, f"X32{g}"], writes=[f"X32{g}"])
            bg_step()
        if stage == 'g4':
            return True
        for g in range(G):
            c_ = c_s[g]
            for h in range(2):
                hs = slice(64 * h, 64 * h + 64)
                b2, b2k = (B[5], "B5") if h == 0 else (B[4], "B4")
                o2 = (256 if h == 0 else 0) + g * 128
                P.op("pe", MM(b2[:, o2:o2 + 128], O["Kt"][hs, c_], O["Rt"][hs, c_]), reads=opk, writes=[b2k])
                P.op("dve", TT(ARK[g][h][:], b2[:, o2:o2 + 128], MIU, ALU.mult), reads=[b2k, "cf"], writes=[f"ARK_{g}{h}"])
        for g in range(G):
            P.op("pe", TR(B6[:, g * 128:(g + 1) * 128], Xbf[:, g, 0:128], identb[:]), reads=[f"Xbf{g}", "identb"], writes=["B6"])
        P.op("act", ACT(WTs[:, :, :].rearrange("p g c -> p (g c)"), B6[:, 0:256], AF.Copy), reads=["B6"], writes=["WTs"])
        if stage == 'g5':
            return True
        for g in range(G):
            c_ = c_s[g]
            ci = tt * 4 + ccs[g]
            uk = f"Ubf{g}"
            Mbf, mbk = Mb2[ci % 2], f"Mb{ci % 2}"
            Mnx, mnk = Mb2[(ci + 1) % 2], f"Mb{(ci + 1) % 2}"
            P.op("pe", MM(B[5][:, 0:128], WTs[:, g, :], Mbf[:]), reads=["WTs", mbk], writes=["B5"])
            for h in range(2):
                hc_ = slice(64 * h, 64 * h + 64)
                P.op("dve", TT(Upad[g][h][:, hc_], B[5][:, hc_], X32[:, g, 128 + 64 * h:128 + 64 * h + 64], ALU.add),
                     reads=["B5", f"X32{g}"], writes=[f"Upad{g}"])
            psM = B[5][:, 128:256]
            P.op("pe", MM(psM, TOK[g][:, 256:384], Upad[g][0][:], start=True, stop=False), reads=[f"TOK{g}", f"Upad{g}"], writes=["B5"])
            P.op("pe", MM(psM, TOK[g][:, 256:384], Upad[g][1][:], start=False, stop=False), reads=[f"TOK{g}", f"Upad{g}"], writes=["B5"])
            P.op("pe", MM(psM, TOK[g][:, 384:512], TOK[g][:, 128:256], start=False, stop=True), reads=[f"TOK{g}"], writes=["B5"])
            psY = B7[:, 256 + g * 128:256 + (g + 1) * 128]
            P.op("pe", MM(psY, Mbf[:], O["Rt"][:, c_], start=True, stop=False), reads=[mbk] + opk, writes=["B7"])
            for h in range(2):
                P.op("pe", MM(psY, Upad[g][h][:], S1[g][h][:, 384:512], start=False, stop=False), reads=[f"Upad{g}", f"S1_{g}{h}"], writes=["B7"])
                P.op("pe", MM(psY, Vpad[g][h][:], ARK[g][h][:], start=False, stop=(h == 1)), reads=[f"Vpad{g}", f"ARK_{g}{h}"], writes=["B7"])
            for h in range(2):
                hs = slice(64 * h, 64 * h + 64)
                P.op("dve", STT(Mnx[hs, hs], M32[hs, hs], gC[hs, ci:ci + 1], B[5][hs, 128 + 64 * h:128 + 64 * h + 64], ALU.mult, ALU.add),
                     reads=["M32", "gC", "B5"], writes=[mnk])
            for h in range(2):
                hs = slice(64 * h, 64 * h + 64)
                P.op("dve", STT(M32[hs, hs], M32[hs, hs], gC[hs, ci:ci + 1], B[5][hs, 128 + 64 * h:128 + 64 * h + 64], ALU.mult, ALU.add),
                     reads=["M32", "gC", "B5"], writes=["M32"])
            P.op("act", ACT(YT[s][:, c_], psY, AF.Copy), reads=["B7"], writes=[f"YT{s}"])
            bg_step()

    def post(hp, tt, s):
        ts = slice(tt * 512, (tt + 1) * 512)
        hc = slice(hp * 128, (hp + 1) * 128)
        col = lambda base: pv[:, base + hp:base + hp + 1]
        yk = f"YT{s}"
        P.op("act", ACT(ybf[:], YT[s][:], AF.Copy), reads=[yk], writes=["kk2"])
        P.op("pe", MM(B[0][:], BOb[:], ybf[:]), reads=["BOb", "kk2"], writes=["B0"])
        P.op("dve", STT(ym[:], B[0][:], -1.0 / 64, YT[s][:], ALU.mult, ALU.add), reads=["B0", yk], writes=["E3"])
        P.op("act", ACT(ybf[:], ym[:], AF.Square), reads=["E3"], writes=["kk2"])
        P.op("pe", MM(B[1][:], BOb[:], ybf[:]), reads=["BOb", "kk2"], writes=["B1"])
        P.op("act", ACT(tm["t1"][:], B[1][:], AF.Ln, scale=1.0 / 64, bias=64e-5), reads=["B1"], writes=["t1"])
        P.op("act", ACT(tm["t1"][:], tm["t1"][:], AF.Exp, scale=-0.5), reads=["t1"], writes=["t1"])
        P.op("dve", TT(ym[:], ym[:], tm["t1"][:], ALU.mult), reads=["E3", "t1"], writes=["E3"])
        P.op("act", ACT(ym[:], ym[:], AF.Identity, scale=col(PV_LW), bias=col(PV_LB)), reads=["E3", "pv"], writes=["E3"])
        P.op("pool", TT(ym[:], ym[:], BON[s][:], ALU.add), reads=["E3", f"BON{s}"], writes=["E3"])
        P.op("pe", MM(B[0][:], g2b[:, hc], SG[:, ts]), reads=["g2b", "SG"], writes=["B0"])
        mk = "MO0"
        P.op("dve", TT(MO[0][:], ym[:], B[0][:], ALU.mult), reads=["E3", "B0"], writes=[mk])
        P.dma("sp", mixT[4 + hp, :, ts], MO[0][:], reads=[mk], writes=[f"mixT{4 + hp}_{tt}"])

    cidx = 0
    for hp in range(n_hp):
        P.phase(f'rwkv_hp{hp}')
        inproj(Wr, "Wr", 0 + hp * 128, Zr, "Zr", PV_MU + hp)
        inproj(Wr, "Wr", 512 + hp * 128, Zk, "Zk", PV_MU + 4 + hp)
        inproj(Wr, "Wr", 1024 + hp * 128, Zv, "Zv", PV_MU + 8 + hp)
        P.op("pool", MS(M32[:], 0.0), writes=["M32"])
        for q_ in range(2):
            P.op("pool", MS(Mb2[q_][:], 0.0), writes=[f"Mb{q_}"])
        bg = []
        BG_K = [4]

        def bg_step():
            P.replay(bg, BG_K[0])

        prep(hp, 0, 0)
        for tt in range(n_tt):
            s = tt % 2
            P.defer_begin()
            if tt > 0:
                post(hp, tt - 1, (tt - 1) % 2)
            if tt + 1 < n_tt:
                prep(hp, tt + 1, (tt + 1) % 2)
            bg.extend(P.defer_end())
            BG_K[0] = max(1, -(-len(bg) // 16))
            for gi in range(2):
                group(hp, tt, gi, s)
            P.replay(bg, len(bg))
        post(hp, n_tt - 1, (n_tt - 1) % 2)
    P.barrier()
    P.release()
    if stage == "rwkv":
        return P
    P.phase('nsa_proj')
    w_gl = nc.dram_tensor("w_gl", [128, 8, 24], F32, kind="ExternalInput").ap()
    gbd = nc.dram_tensor("gb", [128, 24], F32, kind="ExternalInput").ap()
    w1kd = nc.dram_tensor("w1k", [128, 32, 128], F32, kind="ExternalInput").ap()
    w1vd = nc.dram_tensor("w1v", [128, 32, 128], F32, kind="ExternalInput").ap()
    poskd = nc.dram_tensor("posk", [128, 32], F32, kind="ExternalInput").ap()
    posvd = nc.dram_tensor("posv", [128, 32], F32, kind="ExternalInput").ap()
    w2kd = nc.dram_tensor("w2k", [128, 128], F32, kind="ExternalInput").ap()
    w2vd = nc.dram_tensor("w2v", [128, 64], F32, kind="ExternalInput").ap()
    ovlad = nc.dram_tensor("ovla", [128, 2, 65], F32, kind="ExternalInput").ap()
    tzd = nc.dram_tensor("tz", [128, 4480], F32, kind="ExternalInput").ap()
    tzwd = nc.dram_tensor("tzw", [128, 1408], F32, kind="ExternalInput").ap()
    d0d = nc.dram_tensor("d0", [128, 4096], F32, kind="ExternalInput").ap()
    expd = nc.dram_tensor("expand", [128, 4096], F32, kind="ExternalInput").ap()
    kbabd = nc.dram_tensor("kbab", [128, 256], F32, kind="ExternalInput").ap()

    P.mark()
    QT = P.sb([128, 4, T], BF16)
    KSd = [P.sb([128, T], BF16) for _ in range(2)]
    KWd = [P.sb([128, T], BF16) for _ in range(2)]
    VSa = P.sb([128, 32, 2, 65], BF16)
    VWa = P.sb([128, 32, 2, 65], BF16)
    KCCd = [P.sb([128, 256], BF16) for _ in range(2)]
    VCa = P.sb([128, 2, 2, 130], BF16)
    GATES = P.sb([128, 32, 24], F32)
    qg8 = P.sb([128, 1], F32)
    P.op("dve", TS(qg8[:], pv[:, PV_QG:PV_QG + 1], 0.125, ALU.mult), reads=["pv"], writes=["qg8"])
    P.op("pool", MS(VSa[:], 1.0), writes=["VSa"])
    P.op("pool", MS(VWa[:], 1.0), writes=["VWa"])
    for kv in range(2):
        P.op("pool", MS(KCCd[kv][:], 0.0), writes=[f"KCCd{kv}"])

    P.mark()
    wst2 = [P.sb([128, 8, 128], F32) for _ in range(2)]
    wch = [P.sb([128, 8, 128], BF16) for _ in range(2)]
    vtmp = P.sb([128, T], BF16)
    nt32s = [P.sb([128, 512], F32) for _ in range(2)]
    nsqs = [P.sb([128, 512], BF16) for _ in range(2)]
    nsds = [P.sb([128, 512], F32) for _ in range(2)]
    nt32, nsq, nsd = nt32s[0], nsqs[0], nsds[0]
    wctr = [0]

    def load_wchunk(col0):
        i = wctr[0] % 2
        wctr[0] += 1
        P.dma("sp" if i == 0 else "act", wst2[i][:], w_in[:, :, col0:col0 + 128].rearrange("c p n -> p c n"), writes=[f"wst2{i}"])
        P.op("pool", CP(wch[i][:], wst2[i][:]), reads=[f"wst2{i}"], writes=[f"wch{i}"])
        return wch[i], f"wch{i}"

    pend2 = []

    def proj_chunk(col0, sink):
        wt, wk = load_wchunk(col0)
        for tt in range(NT):
            ts = slice(tt * 512, (tt + 1) * 512)
            bi = ip_ctr[0] % 2
            ip_ctr[0] += 1
            bank, bk = B[bi], f"B{bi}"
            for dc in range(8):
                P.op("pe", MM(bank[:], wt[:, dc, :], xn[:, dc, ts], start=dc == 0, stop=dc == 7), reads=[wk, "xn"], writes=[bk])
            while pend2:
                pend2.pop(0)()
            sink(tt, ts, bank, bk)

    def flush2():
        while pend2:
            pend2.pop(0)()

    nctr = [0]

    def normed_sink(dst_fn, dkey, gcol):
        def sink(tt, ts, bank, bk):
            u = nctr[0] % 2
            nctr[0] += 1
            a32, asq, asd = nt32s[u], nsqs[u], nsds[u]
            k32, ksq, ksd = f"nt32_{u}", f"nsq_{u}", f"nsd_{u}"
            b2_, b2k = B[2 + u], f"B{2 + u}"
            P.op("act", ACT(a32[:], bank[:], AF.Copy), reads=[bk], writes=[k32])
            P.op("act", ACT(asq[:], a32[:], AF.Square), reads=[k32], writes=[ksq])

            def second():
                P.op("pe", MM(b2_[:], BOb[:], asq[:]), reads=["BOb", ksq], writes=[b2k])
                P.op("act", ACT(asd[:], b2_[:], AF.Ln, scale=1.0 / 64, bias=1e-6), reads=[b2k], writes=[ksd])
                P.op("act", ACT(asd[:], asd[:], AF.Exp, scale=-0.5), reads=[ksd], writes=[ksd])
                P.op("dve", STT(dst_fn(ts), a32[:], gcol, asd[:], ALU.mult, ALU.mult), reads=[k32, ksd, "pv", "qg8"], writes=[dkey])
            pend2.append(second)
        return sink

    def copy_sink(tt, ts, bank, bk):
        P.op("act", ACT(vtmp[:, ts], bank[:], AF.Copy), reads=[bk], writes=["vtmp"])

    for c in range(4):
        proj_chunk(c * 128, normed_sink(lambda ts, c=c: QT[:, c, ts], "QT", qg8[:, 0:1]))
    for kv in range(2):
        proj_chunk(768 + 128 * kv, normed_sink(lambda ts, kv=kv: KSd[kv][:, ts], f"KSd{kv}", pv[:, PV_KSG:PV_KSG + 1]))
        proj_chunk(1152 + 128 * kv, normed_sink(lambda ts, kv=kv: KWd[kv][:, ts], f"KWd{kv}", pv[:, PV_KWG:PV_KWG + 1]))

    flush2()
    for col0, Va, vk in ((1024, VSa, "VSa"), (1408, VWa, "VWa")):
        proj_chunk(col0, copy_sink)
        for sc in range(32):
            q4 = sc % 4
            P.op("pe", TR(B6[:, q4 * 128:(q4 + 1) * 128], vtmp[:, sc * 128:(sc + 1) * 128], identb[:]), reads=["vtmp", "identb"], writes=["B6"])
            P.op("dve" if sc % 2 == 0 else "act",
                 CP(Va[:, sc, :, 0:64], B6[:, q4 * 128:(q4 + 1) * 128].rearrange("p (a b) -> p a b", b=64)) if sc % 2 == 0 else
                 ACT(Va[:, sc, :, 0:64], B6[:, q4 * 128:(q4 + 1) * 128].rearrange("p (a b) -> p a b", b=64), AF.Copy),
                 reads=["B6"], writes=[vk])

    wgl32 = P.sb([128, 8, 24], F32)
    wglb = P.sb([128, 8, 24], BF16)
    gbt = P.sb([128, 24], F32)
    P.dma("sp", wgl32[:], w_gl, writes=["wgl32"])
    P.dma("sp", gbt[:], gbd, writes=["gbt"])
    P.op("pool", CP(wglb[:], wgl32[:]), reads=["wgl32"], writes=["wglb"])
    for half in range(2):
        bank, bk = B[3 + half], f"B{3 + half}"
        for s16 in range(16):
            sub = half * 16 + s16
            for dc in range(8):
                P.op("pe", MM(bank[:, s16 * 24:(s16 + 1) * 24], xn[:, dc, sub * 128:(sub + 1) * 128], wglb[:, dc, :], start=dc == 0, stop=dc == 7),
                     reads=["xn", "wglb"], writes=[bk])
        for s16 in range(16):
            sub = half * 16 + s16
            P.op("dve", TT(GATES[:, sub, :], bank[:, s16 * 24:(s16 + 1) * 24], gbt[:], ALU.add), reads=[bk, "gbt"], writes=["GATES"])
    P.op("act", ACT(GATES[:], GATES[:], AF.Sigmoid), reads=["GATES"], writes=["GATES"])

    W1b = P.sb([128, 32, 128], BF16)
    posb = P.sb([128, 32], BF16)
    pos32 = P.sb([128, 32], F32)
    w2k32 = P.sb([128, 128], F32)
    w2kb = P.sb([128, 128], BF16)
    w2v32 = P.sb([128, 64], F32)
    w2vb = P.sb([128, 64], BF16)
    cbias = P.sb([128, 1], F32)
    GH = [P.sb([128, 256], BF16) for _ in range(2)]
    ovl32 = P.sb([128, 2, 65], F32)
    P.dma("sp", w2k32[:], w2kd, writes=["w2k32"])
    P.dma("sp", w2v32[:], w2vd, writes=["w2v32"])
    P.dma("sp", ovl32[:], ovlad, writes=["ovl32"])
    P.op("pool", CP(w2kb[:], w2k32[:]), reads=["w2k32"], writes=["w2kb"])
    P.op("pool", CP(w2vb[:], w2v32[:]), reads=["w2v32"], writes=["w2vb"])
    for kv in range(2):
        P.op("pool", MS(GH[kv][:], 0.0), writes=[f"GH{kv}"])
        for n2 in range(2):
            P.op("pool", CP(VCa[:, n2, kv, 64:129], ovl32[:, n2, :]), reads=["ovl32"], writes=["VCa"])

    for which, col0, w1d, posd in (("k", 512, w1kd, poskd), ("v", 640, w1vd, posvd)):
        proj_chunk(col0, copy_sink)
        for q in range(4):
            i = wctr[0] % 2
            wctr[0] += 1
            P.dma("sp" if i == 0 else "act", wst2[i][:], w1d[:, q * 8:(q + 1) * 8, :], writes=[f"wst2{i}"])
            P.op("pool", CP(W1b[:, q * 8:(q + 1) * 8, :], wst2[i][:]), reads=[f"wst2{i}"], writes=["W1b"])
        P.dma("sp", pos32[:], posd, writes=["pos32"])
        P.op("pool", CP(posb[:], pos32[:]), reads=["pos32"], writes=["posb"])
        for l in range(32):
            P.op("pe", MM(B[2][:, 0:1], W1b[0:64, l, :], posb[0:64, l:l + 1], start=l == 0, stop=l == 31), reads=["W1b", "posb"], writes=["B2"])
        P.op("dve", CP(cbias[:], B[2][:, 0:1]), reads=["B2"], writes=["cbias"])
        for kv in range(2):
            hs = slice(64 * kv, 64 * kv + 64)
            bank, bk = B[3 + kv], f"B{3 + kv}"
            for l in range(32):
                P.op("pe", MM(bank[:, 0:255], W1b[hs, l, :], vtmp[hs, l:l + 16 * 254 + 1:16], start=l == 0, stop=l == 31),
                     reads=["W1b", "vtmp"], writes=[bk])
            gx, gu = nt32[:, 0:255], nsd[:, 0:255]
            P.op("act", ACT(gx, bank[:, 0:255], AF.Identity, bias=cbias[:, 0:1]), reads=[bk, "cbias"], writes=["nt32_0"])
            P.op("act", ACT(gu, gx, AF.Square), reads=["nt32_0"], writes=["nsd_0"])
            P.op("dve", TS(gu, gu, 0.044715, ALU.mult, 1.0, ALU.add), reads=["nsd_0"], writes=["nsd_0"])
            P.op("dve", TT(gu, gu, gx, ALU.mult), reads=["nsd_0", "nt32_0"], writes=["nsd_0"])
            P.op("act", ACT(gu, gu, AF.Tanh, scale=0.7978845608028654), reads=["nsd_0"], writes=["nsd_0"])
            P.op("dve", STT(gu, gu, 1.0, gx, ALU.add, ALU.mult), reads=["nsd_0", "nt32_0"], writes=["nsd_0"])
            P.op("dve", TS(GH[kv][:, 0:255], gu, 0.5, ALU.mult), reads=["nsd_0"], writes=[f"GH{kv}"])
        for kv in range(2):
            if which == "k":
                P.op("pe", MM(B[2][:, 0:256], w2kb[:], GH[kv][:]), reads=["w2kb", f"GH{kv}"], writes=["B2"])
                P.op("act", ACT(nt32[:, 0:256], B[2][:, 0:256], AF.Copy), reads=["B2"], writes=["nt32_0"])
                P.op("act", ACT(nsq[:, 0:256], nt32[:, 0:256], AF.Square), reads=["nt32_0"], writes=["nsq_0"])
                P.op("pe", MM(B[2][:, 0:256], BOb[:], nsq[:, 0:256]), reads=["BOb", "nsq_0"], writes=["B2"])
                P.op("act", ACT(nsd[:, 0:256], B[2][:, 0:256], AF.Ln, scale=1.0 / 64, bias=1e-6), reads=["B2"], writes=["nsd_0"])
                P.op("act", ACT(nsd[:, 0:256], nsd[:, 0:256], AF.Exp, scale=-0.5), reads=["nsd_0"], writes=["nsd_0"])
                P.op("dve", STT(KCCd[kv][:, 0:255], nt32[:, 0:255], pv[:, PV_KCG:PV_KCG + 1], nsd[:, 0:255], ALU.mult, ALU.mult),
                     reads=["nt32_0", "nsd_0", "pv"], writes=[f"KCCd{kv}"])
            else:
                for n2 in range(2):
                    P.op("pe", MM(B[2][:, n2 * 64:(n2 + 1) * 64], GH[kv][:, n2 * 128:(n2 + 1) * 128], w2vb[:]), reads=["w2vb", f"GH{kv}"], writes=["B2"])
                for n2 in range(2):
                    P.op("dve", CP(VCa[:, n2, kv, 0:64], B[2][:, n2 * 64:(n2 + 1) * 64]), reads=["B2"], writes=["VCa"])
    P.barrier()
    P.release()
    P.sb_limit = 229312

    P.phase('nsa_att')
    Tz = P.sb([128, 4480], F32)
    TzW = P.sb([128, 1408], F32)
    D0 = P.sb([128, 4096], F32)
    kbab = P.sb([128, 256], F32)
    P.dma("sp", Tz[:], tzd, writes=["Tz"])
    P.dma("act", TzW[:], tzwd, writes=["TzW"])
    P.dma("sp", kbab[:], kbabd, writes=["kbab"])
    LSE = [[KSd[kv], P.sb([128, T], BF16)] for kv in range(2)]
    P.mark()
    est = P.sb([128, 4096], F32)
    P.dma("act", est[:], expd, writes=["est"])
    for kv in range(2):
        P.op("pool", CP(LSE[kv][1][64:128, :], KSd[kv][64:128, :]), reads=[f"KSd{kv}"], writes=[f"LSE{kv}1"])
        P.op("pool" if kv == 0 else "dve", CP(LSE[kv][1][0:64, :], est[0:64, :]), reads=["est"], writes=[f"LSE{kv}1"])
        P.op("dve" if kv == 0 else "pool", CP(KSd[kv][64:128, :], est[64:128, :]), reads=["est", f"LSE{kv}1"], writes=[f"KSd{kv}"])
    P.barrier()
    P.release()
    P.dma("sp", D0[:], d0d, writes=["D0"])

    NSB = 5
    SBK = [(B[0], "B0"), (B[1], "B1"), (B7, "B7"), (B[4], "B4"), (B[5], "B5")]
    PVB = [(B[2], "B2"), (B[3], "B3")]
    stmp = [P.sb([128, 512], F32) for _ in range(NSB)]
    PT = [P.sb([128, 512], BF16) for _ in range(NSB)]
    acc = P.sb([128, 4, 512], F32)
    accb = P.sb([128, 4, 512], BF16)
    imp = P.sb([128, 2, 4, 64], F32)
    etmp = P.sb([128, 4, 64], F32)
    iw = P.sb([128, 64], F32)
    iw2 = P.sb([128, 64], F32)
    m8 = P.sb([128, 16], F32)
    selb2 = P.sb([128, 128], BF16)
    SELBT = [P.sb([128, 512], BF16) for _ in range(2)]
    RS = [P.sb([128, 512], BF16) for _ in range(2)]
    rsc = [0]
    sm = [P.sb([128, 12], F32) for _ in range(2)]
    MXo = [P.sb([128, 512], BF16) for _ in range(2)]
    sctr = [0]
    pvc = [0]
    mxc = [0]
    ectr = [0]
    pend = []
    LOOK = 4

    def push(stage_a, stage_b):
        stage_a()
        pend.append(stage_b)
        if len(pend) > LOOK:
            pend.pop(0)()

    def flush():
        while pend:
            pend.pop(0)()

    def attend(tt, h, br, kT, kkey, Va, vkey, chunks, bias_ap, bkey, sub_range, nvc, sel=False, first_head=False):
        ts = slice(tt * 512, (tt + 1) * 512)
        kvh, qc = h // 4, h // 2
        qs = slice(64 * (h % 2), 64 * (h % 2) + 64)
        m_h = SLOPES[h]
        hcols = slice(h * 64, (h + 1) * 64)
        if nvc > 65:
            pb = [PVB[(pvc[0] + q) % len(PVB)] for q in range(2)]
            pvc[0] += 2
            reg = lambda j: (pb[j // 2][0][:, (j % 2) * 256:(j % 2) * 256 + nvc], pb[j // 2][1], pb[j // 2][0], (j % 2) * 256)
        else:
            pb = [PVB[pvc[0] % len(PVB)]]
            pvc[0] += 1
            reg = lambda j: (pb[0][0][:, j * 128:j * 128 + nvc], pb[0][1], pb[0][0], j * 128)
        if sel:
            ri = rsc[0] % 2
            rsc[0] += 1
            rk = f"RS{ri}"
            os_ = slice(64 * (1 - h % 2), 64 * (1 - h % 2) + 64)
            P.op("pool", CP(RS[ri][qs, :], QT[qs, qc, ts]), reads=["QT"], writes=[rk])
            P.op("pool", CP(RS[ri][os_, :], SELBT[kvh][os_, :]), reads=[f"SELBT{kvh}"], writes=[rk])
        started = set()
        pv_list = [(sc, j) for sc in chunks for j in range(4) if sub_range(j)[0] <= sc <= sub_range(j)[1]]
        last_for_bank = {}
        for sc, j in pv_list:
            last_for_bank[reg(j)[1]] = (sc, j)

        def make(sc, is_last):
            i = sctr[0] % NSB
            sctr[0] += 1
            bank, bk = SBK[i]

            def stage_a():
                if sel:
                    P.op("pe", MM(bank[:], LSE[kvh][h % 2][:, sc * 128:(sc + 1) * 128], RS[ri][:, :]),
                         reads=[f"KSd{kvh}", f"LSE{kvh}1", rk], writes=[bk])
                else:
                    P.op("pe", MM(bank[:], kT[qs, sc * 128:(sc + 1) * 128], QT[qs, qc, ts]), reads=[kkey, "QT"], writes=[bk])
                P.op("dve", STT(stmp[i][:], bias_ap(sc), m_h, bank[:], ALU.mult, ALU.add), reads=[bkey, bk], writes=[f"stmp{i}"])
                P.op("act", ACT(PT[i][:], stmp[i][:], AF.Exp), reads=[f"stmp{i}"], writes=[f"PT{i}"])

            def stage_b():
                for j in range(4):
                    lo, hi = sub_range(j)
                    if sc < lo or sc > hi:
                        continue
                    out_ap, pk, _, _ = reg(j)
                    P.op("pe", MM(out_ap, PT[i][:, j * 128:(j + 1) * 128], Va(sc, kvh), start=(pk not in started),
                                  stop=(last_for_bank[pk] == (sc, j))),
                         reads=[f"PT{i}", vkey], writes=[pk])
                    started.add(pk)
                if is_last:
                    epilogue()
            return stage_a, stage_b

        def epilogue():
            e = ectr[0] % 2
            ectr[0] += 1
            smk = f"sm{e}"
            S = sm[e]
            pks = sorted({reg(j)[1] for j in range(4)})
            if nvc > 65:
                for q in range(2):
                    v2 = pb[q][0][:, :].rearrange("p (j c) -> p j c", c=256)
                    P.op("dve", TS(S[:, 2 * q:2 * q + 2], v2[:, :, 64], 1e-30, ALU.max), reads=pks, writes=[smk])
            else:
                bv = pb[0][0][:, :].rearrange("p (j c) -> p j c", c=128)
                P.op("dve", TS(S[:, 0:4], bv[:, :, 64], 1e-30, ALU.max), reads=pks, writes=[smk])
            P.op("dve", RCP(S[:, 4:8], S[:, 0:4]), reads=[smk], writes=[smk])
            P.op("dve", TT(S[:, 8:12], S[:, 4:8], GATES[:, 4 * tt:4 * tt + 4, 3 * h + br], ALU.mult), reads=[smk, "GATES"], writes=[smk])
            if nvc > 65:
                for q in range(2):
                    v2 = pb[q][0][:, :].rearrange("p (j c) -> p j c", c=256)
                    pk = pb[q][1]
                    fb2 = S[:, 8 + 2 * q:10 + 2 * q].unsqueeze(2).to_broadcast([128, 2, 64])
                    rb2 = S[:, 4 + 2 * q:6 + 2 * q].unsqueeze(2).to_broadcast([128, 2, 64])
                    P.op("dve", TT(acc[:, 2 * q:2 * q + 2, hcols], v2[:, :, 0:64], fb2, ALU.mult), reads=[pk, smk], writes=["acc"])
                    if first_head:
                        P.op("dve", TT(imp[:, kvh, 2 * q:2 * q + 2, :], v2[:, :, 65:129], rb2, ALU.mult), reads=[pk, smk], writes=["imp"])
                    else:
                        P.op("dve", TT(etmp[:, 2 * q:2 * q + 2, :], v2[:, :, 65:129], rb2, ALU.mult), reads=[pk, smk], writes=["etmp"])
                        P.op("pool", TT(imp[:, kvh, 2 * q:2 * q + 2, :], imp[:, kvh, 2 * q:2 * q + 2, :], etmp[:, 2 * q:2 * q + 2, :], ALU.add),
                             reads=["etmp", "imp"], writes=["imp"])
            else:
                bv = pb[0][0][:, :].rearrange("p (j c) -> p j c", c=128)
                fb = S[:, 8:12].unsqueeze(2).to_broadcast([128, 4, 64])
                P.op("dve", TT(etmp[:], bv[:, :, 0:64], fb, ALU.mult), reads=pks + [smk], writes=["etmp"])
                P.op("pool", TT(acc[:, :, hcols], acc[:, :, hcols], etmp[:], ALU.add), reads=["etmp", "acc"], writes=["acc"])

        for n_, sc in enumerate(chunks):
            a_, b_ = make(sc, n_ == len(chunks) - 1)
            push(a_, b_)

    for tt in range(n_att):
        ts = slice(tt * 512, (tt + 1) * 512)
        ncs = [0, 1] if tt >= 4 else [0]
        for h in range(8):
            attend(tt, h, 0, KCCd[h // 4], f"KCCd{h // 4}", lambda sc, kvh: VCa[:, sc, kvh, 0:129], "VCa", ncs,
                   lambda sc: D0[:, tt * 512 - 2048 * sc:tt * 512 - 2048 * sc + 512], "D0",
                   lambda j: (0, ncs[-1]), 129, first_head=(h % 4 == 0))
        flush()
        P.phase(f'att{tt}_topk')
        for kvh in range(2):
            for j in range(4):
                sub = 4 * tt + j
                o_ = 62 - 2 * sub
                P.op("dve", TT(iw[:], imp[:, kvh, j, :], kbab[:, o_:o_ + 64], ALU.mult), reads=["imp", "kbab"], writes=["iw"])
                P.op("dve", TT(iw[:], iw[:], kbab[:, 128 + o_:128 + o_ + 64], ALU.add), reads=["iw", "kbab"], writes=["iw"])
                P.op("dve", MS(iw[:, 0:1], 1.0e6), reads=["iw"], writes=["iw"])
                P.op("dve", lambda e: e.max(out=m8[:, 0:8], in_=iw[:]), reads=["iw"], writes=["m8"])
                P.op("dve", lambda e: e.match_replace(out=iw2[:], in_to_replace=m8[:, 0:8], in_values=iw[:], imm_value=-3.0e38),
                     reads=["iw", "m8"], writes=["iw2"])
                P.op("dve", lambda e: e.max(out=m8[:, 8:16], in_=iw2[:]), reads=["iw2"], writes=["m8"])
                for dup in range(2):
                    P.op("dve", TS(selb2[:, dup * 64:(dup + 1) * 64], iw[:], m8[:, 15:16], ALU.is_lt, -30000.0, ALU.mult),
                         reads=["iw", "m8"], writes=["selb2"])
                P.op("pe", TR(B6[:, j * 128:(j + 1) * 128], selb2[:], identb[:]), reads=["selb2", "identb"], writes=["B6"])
            P.op("act", ACT(SELBT[kvh][:], B6[:, 0:512], AF.Copy), reads=["B6"], writes=[f"SELBT{kvh}"])
        P.phase(f'att{tt}_sel')
        for h in range(8):
            attend(tt, h, 1, KSd[h // 4], f"KSd{h // 4}", lambda sc, kvh: VSa[:, sc, kvh, :], "VSa", list(range(0, 4 * tt + 4)),
                   lambda sc: Tz[:, 512 * tt - 128 * sc + 384:512 * tt - 128 * sc + 384 + 512], "Tz",
                   lambda j: (0, 4 * tt + j), 65, sel=True)
        P.phase(f'att{tt}_win')
        for h in range(8):
            attend(tt, h, 2, KWd[h // 4], f"KWd{h // 4}", lambda sc, kvh: VWa[:, sc, kvh, :], "VWa", list(range(max(0, 4 * tt - 4), 4 * tt + 4)),
                   lambda sc: TzW[:, 512 * tt - 128 * sc + 384:512 * tt - 128 * sc + 384 + 512], "TzW",
                   lambda j: (max(0, 4 * tt + j - 4), 4 * tt + j), 65)
        flush()
        P.op("act", ACT(accb[:], acc[:], AF.Copy), reads=["acc"], writes=["accb"])
        for c in range(4):
            for j in range(4):
                P.op("pe", TR(B6[:, j * 128:(j + 1) * 128], accb[:, j, c * 128:(c + 1) * 128], identb[:]), reads=["accb", "identb"], writes=["B6"])
            i = mxc[0] % 2
            mxc[0] += 1
            P.op("dve", CP(MXo[i][:], B6[:, 0:512]), reads=["B6"], writes=[f"MXo{i}"])
            P.dma("sp", mixT[c, :, ts], MXo[i][:], reads=[f"MXo{i}"], writes=[f"mixT{c}_{tt}"])
    P.barrier()
    P.release()
    if stage == "nsa":
        return P
    P.phase('ffn_w')
    w_outd = nc.dram_tensor("w_out", [8, 128, 1024], F32, kind="ExternalInput").ap()
    w_ff1d = nc.dram_tensor("w_ff1", [8, 128, 4096], F32, kind="ExternalInput").ap()
    w_ff2d = nc.dram_tensor("w_ff2", [32, 128, 1024], F32, kind="ExternalInput").ap()
    outT = nc.dram_tensor("outT", [8, 128, T], F32, kind="ExternalOutput").ap()
    P.mark()
    Wo = P.sb([128, 8, 1024], BF16)
    W1f = P.sb([128, 8, 4096], BF16)
    W2f = P.sb([128, 32, 1024], BF16)
    Hh = P.sb([128, 32, 256], BF16)
    _hb = P.sb_off - 16384
    fst = [P.sb_at([128, 2048], F32, _hb + i_ * 8192) for i_ in range(2)]
    fctr = [0]
    cast_eng = ["pool", "dve", "act"]

    def load_cast(dst, src, view=None, wkey="Wf"):
        i = fctr[0] % 2
        e = cast_eng[fctr[0] % 3]
        fctr[0] += 1
        sv = fst[i][:, :] if view is None else view(fst[i])
        P.dma("sp" if i == 0 else "act", sv, src, writes=[f"fst{i}"])
        if e == "act":
            P.op("act", ACT(dst, sv, AF.Copy), reads=[f"fst{i}"], writes=[wkey])
        else:
            P.op(e, CP(dst, sv), reads=[f"fst{i}"], writes=[wkey])

    for dc in range(4):
        load_cast(Wo[:, 2 * dc:2 * dc + 2, :], w_outd[2 * dc:2 * dc + 2].rearrange("a p b -> p a b"),
                  view=lambda t_: t_[:, :].rearrange("p (a b) -> p a b", a=2), wkey="Wo")
    for dc in range(8):
        for hf in range(2):
            load_cast(W1f[:, dc, hf * 2048:(hf + 1) * 2048], w_ff1d[dc, :, hf * 2048:(hf + 1) * 2048], wkey="W1f")
    for f2 in range(16):
        load_cast(W2f[:, 2 * f2:2 * f2 + 2, :], w_ff2d[2 * f2:2 * f2 + 2].rearrange("a p b -> p a b"),
                  view=lambda t_: t_[:, :].rearrange("p (a b) -> p a b", a=2), wkey="W2f")
    P.phase('ffn')
    TW = 256
    xin3 = [P.sb([128, 8, TW], F32) for _ in range(2)]
    MX = [P.sb([128, 8, TW], BF16) for _ in range(2)]
    sq3 = P.sb([128, 8, TW], BF16)
    sd3 = P.sb([128, TW], F32)
    xn1 = P.sb([128, 8, TW], BF16)
    rl = [P.sb([128, TW], F32) for _ in range(2)]
    ot = [P.sb([128, TW], F32) for _ in range(2)]
    bctr = [0]

    def nb():
        i = bctr[0] % 6
        bctr[0] += 1
        return B[i], f"B{i}"

    for t3 in range(T // TW):
        ts = slice(t3 * TW, (t3 + 1) * TW)
        p = t3 % 2
        xk, mk = f"xin3{p}", f"MX{p}"
        P.dma("sp", xin3[p][:], xT[:, :, ts].rearrange("c p t -> p c t"), writes=[xk])
        P.dma("act", MX[p][:], mixT[:, :, ts].rearrange("c p t -> p c t"), reads=[f"mixT{c}_{t3 * TW // 512}" for c in range(8)], writes=[mk])
        for dch in range(8):
            bank, bk = nb()
            for c in range(8):
                P.op("pe", MM(bank[:, 0:TW], Wo[:, c, dch * 128:(dch + 1) * 128], MX[p][:, c, :], start=c == 0, stop=c == 7), reads=["Wo", mk], writes=[bk])
            P.op("dve", TT(xin3[p][:, dch, :], bank[:, 0:TW], xin3[p][:, dch, :], ALU.add), reads=[bk, xk], writes=[xk])
        P.op("act", ACT(sq3[:], xin3[p][:], AF.Square), reads=[xk], writes=["sq3"])
        for dc in range(8):
            P.op("pe", MM(B7[:, 0:TW], onesb[:], sq3[:, dc, :], start=dc == 0, stop=dc == 7), reads=["sq3", "onesb"], writes=["B7"])
        P.op("act", ACT(sd3[:], B7[:, 0:TW], AF.Ln, scale=1.0 / 1024, bias=1e-6), reads=["B7"], writes=["sd3"])
        P.op("act", ACT(sd3[:], sd3[:], AF.Exp, scale=-0.5), reads=["sd3"], writes=["sd3"])
        for dc in range(8):
            P.op("dve", STT(xn1[:, dc, :], xin3[p][:, dc, :], pv[:, PV_G2 + dc:PV_G2 + dc + 1], sd3[:], ALU.mult, ALU.mult),
                 reads=[xk, "pv", "sd3"], writes=["xn1"])
        for f in range(32):
            bank, bk = nb()
            for dc in range(8):
                P.op("pe", MM(bank[:, 0:TW], W1f[:, dc, f * 128:(f + 1) * 128], xn1[:, dc, :], start=dc == 0, stop=dc == 7), reads=["W1f", "xn1"], writes=[bk])
            r = rl[f % 2]
            P.op("act", ACT(r[:], bank[:, 0:TW], AF.Relu), reads=[bk], writes=[f"rl{f % 2}"])
            P.op("pool", TT(Hh[:, f, :], r[:], r[:], ALU.mult), reads=[f"rl{f % 2}"], writes=["Hh", "fst0", "fst1"])
        for dch in range(8):
            bank, bk = nb()
            for f in range(32):
                P.op("pe", MM(bank[:, 0:TW], W2f[:, f, dch * 128:(dch + 1) * 128], Hh[:, f, :], start=f == 0, stop=f == 31), reads=["W2f", "Hh"], writes=[bk])
            o = ot[dch % 2]
            P.op("dve", TT(o[:], bank[:, 0:TW], xin3[p][:, dch, :], ALU.add), reads=[bk, xk], writes=[f"ot{dch % 2}"])
            P.dma("sp" if dch % 2 == 0 else "act", outT[dch, :, ts], o[:], reads=[f"ot{dch % 2}"])
    P.barrier()
    P.release()
    return P


def host_inputs(inputs, b):
    f = lambda a: np.ascontiguousarray(a, dtype=np.float32)
    x = inputs["x"][b]
    d = {}
    d["xT"] = f(x.T.reshape(8, 128, T))
    w = np.asarray(inputs["w_in"][0], np.float32)
    wp = np.zeros((1024, WCOLS), np.float32)
    wp[:, 0:512] = w[:, 0:512]
    wp[:, 512:640] = w[:, 512:640]
    wp[:, 640:768] = w[:, 640:768]
    for kv in range(2):
        wp[:, 768 + 128 * kv:768 + 128 * kv + 64] = w[:, 768 + 64 * kv:768 + 64 * kv + 64]
        wp[:, 768 + 128 * kv + 64:768 + 128 * kv + 128] = w[:, 768 + 64 * kv:768 + 64 * kv + 64]
        wp[:, 1152 + 128 * kv:1152 + 128 * kv + 64] = w[:, 1024 + 64 * kv:1024 + 64 * kv + 64]
        wp[:, 1152 + 128 * kv + 64:1152 + 128 * kv + 128] = w[:, 1024 + 64 * kv:1024 + 64 * kv + 64]
    wp[:, 1024:1152] = w[:, 896:1024]
    wp[:, 1408:1536] = w[:, 1152:1280]
    wp[:, RW0:RW0 + 1792] = w[:, 1304:3096]
    d["w_in"] = f(wp.reshape(8, 128, WCOLS))
    d["w_gl"] = f(w[:, 1280:1304].reshape(8, 128, 24).transpose(1, 0, 2))
    pv = np.zeros((128, NPV), np.float32)
    pv[:, PV_G:PV_G + 8] = inputs["ln_mix_g"][0].reshape(8, 128).T
    pv[:, PV_MU:PV_MU + 14] = inputs["rwkv_mu"][0].reshape(14, 128).T
    for nm, c in (("rwkv_w0", PV_W0), ("rwkv_a0", PV_A0), ("rwkv_k_k", PV_KK), ("rwkv_k_a", PV_KA),
                  ("rwkv_lnx_w", PV_LW), ("rwkv_lnx_b", PV_LB)):
        pv[:, c:c + 4] = inputs[nm][0].reshape(4, 128).T
    pv[:, PV_RK:PV_RK + 4] = inputs["rwkv_r_k"][0].reshape(4, 128).T
    d["pvec"] = pv
    d["w2a2"] = f(np.concatenate([inputs["rwkv_w2"][0], inputs["rwkv_a2"][0]], axis=0))
    d["g2"] = f(inputs["rwkv_g2"][0])
    o = np.ones((128, 128), np.float32)
    msl, msu, miu = np.tril(o, -1), np.triu(o, 1), np.triu(o, 0)
    bo = np.kron(np.eye(2, dtype=np.float32), np.ones((64, 64), np.float32))
    d["cst"] = f(np.concatenate([msl, msu, msu, miu, miu, np.eye(128, dtype=np.float32), bo], axis=1))
    t64 = lambda v: np.tile(np.asarray(v, np.float32), 2)
    pv[:, PV_QG] = t64(inputs["q_norm_g"][0])
    pv[:, PV_KSG] = t64(inputs["ks_norm_g"][0])
    pv[:, PV_KWG] = t64(inputs["kw_norm_g"][0])
    pv[:, PV_KCG] = t64(inputs["kc_norm_g"][0])
    pv[:, PV_G2:PV_G2 + 8] = inputs["ln_ffn_g"][0].reshape(8, 128).T
    d["gb"] = f(np.tile(inputs["nsa_gate_b"][0][None, :], (128, 1)))
    for nm, src in (("w1k", "cmp_k_w1"), ("w1v", "cmp_v_w1")):
        w1 = np.asarray(inputs[src][0], np.float32).reshape(32, 64, 128).transpose(1, 0, 2)
        d[nm] = f(np.concatenate([w1, w1], axis=0))
    for nm, src in (("posk", "cmp_k_pos"), ("posv", "cmp_v_pos")):
        pt = np.asarray(inputs[src][0], np.float32).T
        d[nm] = f(np.concatenate([pt, pt], axis=0))
    w2k = np.asarray(inputs["cmp_k_w2"][0], np.float32)
    d["w2k"] = f(np.concatenate([w2k, w2k], axis=1))
    d["w2v"] = f(inputs["cmp_v_w2"][0])
    n_ = np.arange(256)[:, None]
    j_ = np.arange(64)[None, :]
    ovl = ((16 * n_ <= 64 * j_ + 63) & (16 * n_ + 31 >= 64 * j_)).astype(np.float32)
    ovl[255] = 0
    on = np.ones((256, 1), np.float32)
    on[255] = 0
    d["ovla"] = f(np.concatenate([on, ovl], axis=1).reshape(2, 128, 65).transpose(1, 0, 2))
    p_ = np.arange(128)[:, None].astype(np.float64)
    jx = np.arange(4480)[None, :] - 384 - p_
    d["tz"] = f(np.where(jx >= 0, -jx, NEGB))
    jw = np.arange(1408)[None, :] - 384 - p_
    d["tzw"] = f(np.where((jw >= 0) & (jw < 512), -jw, NEGB))
    dd = np.arange(4096)[None, :] - 16 * p_ - 31
    d["d0"] = f(np.where(dd >= 0, -dd, NEGB))
    d["expand"] = f((np.arange(4096)[None, :] // 64 == (np.arange(128)[:, None] % 64)))
    y_ = np.arange(128)[None, :]
    cpos = np.where(np.arange(128)[:, None] < 64, 62, 63)
    kb = np.where(y_ >= cpos - 1, 0.0, 1.0)
    ab = np.where(y_ > cpos, -1.0e30, np.where(y_ >= cpos - 1, 1.0e6, 0.0))
    d["kbab"] = f(np.concatenate([kb, ab], axis=1))
    d["w_out"] = f(np.asarray(inputs["w_out"][0]).reshape(8, 128, 1024))
    d["w_ff1"] = f(np.asarray(inputs["w_ff1"][0]).reshape(8, 128, 4096))
    d["w_ff2"] = f(np.asarray(inputs["w_ff2"][0]).reshape(32, 128, 1024))
    return d


_CACHE = {}


def kernel(**inputs):
    inputs = {k: np.asarray(v) for k, v in inputs.items()}
    if "nc" not in _CACHE:
        _CACHE["nc"] = build().emit()
    nc = _CACHE["nc"]
    maps = [host_inputs(inputs, b) for b in range(8)]
    res = run_bass_kernel_spmd(nc, maps, core_ids=list(range(8)))
    out = np.empty((8, T, 1024), np.float32)
    for b in range(8):
        out[b] = np.asarray(res.results[b]["outT"], np.float32).reshape(1024, T).T
    return out
```

```python
import numpy as np
import concourse.bass as bass
import concourse.mybir as mybir
from concourse.bass_utils import run_bass_kernel_spmd

F32 = mybir.dt.float32
BF16 = mybir.dt.bfloat16
AF = mybir.ActivationFunctionType
ALU = mybir.AluOpType

ENGS = ["pe", "act", "dve", "pool", "sp"]
T = 4096
NT = 8
DEC = 0.6065306597126334


class Prog:
    NSLOT = 12

    def __init__(self):
        self.nc = bass.Bass("TRN2", target_bir_lowering=False)
        self.ops = {e: [] for e in ENGS}
        self.count = {e: 0 for e in ENGS}
        self.last_w = {}
        self.readers = {}
        self.waited = {e: {} for e in ENGS}
        self.slot_next = {e: 0 for e in ENGS}
        self.slot_gen = {}
        self.sb_off = 16512
        self.sb_limit = 229312
        self.sb_marks = []
        self.n_t = 0
        self.all_tokens = {}

    def sb(self, shape, dtype, name=None):
        esz = {F32: 4, BF16: 2}[dtype]
        nbytes = int(np.prod(shape[1:])) * esz
        off = (self.sb_off + 63) // 64 * 64
        self.n_t += 1
        t = self.nc.alloc_sbuf_tensor_at(name or f"t{self.n_t}", list(shape), dtype, offset=off)
        self.sb_off = off + nbytes
        self.sb_peak = max(getattr(self, "sb_peak", 0), self.sb_off)
        assert self.sb_off <= self.sb_limit, f"SBUF overflow {self.sb_off} > {self.sb_limit}"
        return t

    def sb_at(self, shape, dtype, off, name=None):
        self.n_t += 1
        return self.nc.alloc_sbuf_tensor_at(name or f"t{self.n_t}", list(shape), dtype, offset=off)

    def mark(self):
        self.sb_marks.append(self.sb_off)

    def release(self):
        self.sb_off = self.sb_marks.pop()

    def _deps(self, eng, reads, writes):
        toks = {}

        def add(tok):
            if tok is None:
                return
            s, v = tok
            if toks.get(s, 0) < v:
                toks[s] = v

        for r in reads:
            add(self.last_w.get(r))
        for w in writes:
            add(self.last_w.get(w))
            for s, v in self.readers.get(w, {}).items():
                add((s, v))
        out = []
        for s, v in toks.items():
            if s == ("c", "pe") and eng == "pe":
                continue
            if self.waited[eng].get(s, 0) >= v:
                continue
            self.waited[eng][s] = v
            out.append((s, v))
        return out

    def _commit(self, tok, reads, writes):
        s, v = tok
        self.all_tokens[s] = max(self.all_tokens.get(s, 0), v)
        for r in reads:
            d = self.readers.setdefault(r, {})
            if d.get(s, 0) < v:
                d[s] = v
        for w in writes:
            self.last_w[w] = tok
            self.readers[w] = {}

    @staticmethod
    def _is_psum(k):
        return isinstance(k, str) and len(k) == 2 and k[0] == "B" and k[1].isdigit()

    def defer_begin(self):
        self._defer = []

    def defer_end(self):
        lst, self._defer = self._defer, None
        return lst

    def replay(self, lst, n):
        for _ in range(min(n, len(lst))):
            kind, args = lst.pop(0)
            (self.op if kind == "op" else self.dma)(*args)

    def op(self, eng, fn, reads=(), writes=()):
        if getattr(self, "_defer", None) is not None:
            self._defer.append(("op", (eng, fn, list(reads), list(writes))))
            return
        writes = list(writes) + [k for k in reads if self._is_psum(k) and k not in writes]
        waits = self._deps(eng, reads, writes)
        self.count[eng] += 1
        tok = (("c", eng), self.count[eng])
        self.ops[eng].append((waits, fn, tok))
        self._commit(tok, reads, writes)

    def dma(self, eng, out, in_, reads=(), writes=()):
        if getattr(self, "_defer", None) is not None:
            self._defer.append(("dma", (eng, out, in_, list(reads), list(writes))))
            return
        i = self.slot_next[eng]
        self.slot_next[eng] = (i + 1) % self.NSLOT
        s = ("d", eng, i)
        gen = self.slot_gen.get(s, 0)
        waits = self._deps(eng, reads, writes)
        if gen > 0 and self.waited[eng].get(s, 0) < 16 * gen:
            self.waited[eng][s] = 16 * gen
            waits.append((s, 16 * gen))
        self.slot_gen[s] = gen + 1
        tok = (s, 16 * (gen + 1))
        self.ops[eng].append((waits, lambda e: e.dma_start(out=out, in_=in_), tok))
        self._commit(tok, reads, writes)

    def phase(self, label):
        if not hasattr(self, 'marks'):
            self.marks = []
        self.marks.append((label, dict(self.count)))

    def barrier(self):
        for e in ENGS:
            waits = []
            for s, v in self.all_tokens.items():
                if s == ("c", e):
                    continue
                if self.waited[e].get(s, 0) >= v:
                    continue
                self.waited[e][s] = v
                waits.append((s, v))
            if waits:
                self.ops[e].append((waits, None, None))

    def emit(self):
        nc = self.nc
        from contextlib import ExitStack

        with ExitStack() as es:
            sems = {}
            for e in ENGS:
                sems[("c", e)] = es.enter_context(nc.semaphore(f"c_{e}"))
            for s in self.slot_gen:
                sems[s] = es.enter_context(nc.semaphore(f"d_{s[1]}_{s[2]}"))
            self.barrier()
            block = es.enter_context(nc.Block())

            def run(e, engine):
                for waits, fn, tok in self.ops[e]:
                    for s, v in waits:
                        engine.wait_ge(sems[s], v)
                    if fn is None:
                        continue
                    ins = fn(engine)
                    ins.then_inc(sems[tok[0]], 16 if tok[0][0] == "d" else 1)

            @block.tensor
            def _(eng):
                run("pe", eng)

            @block.scalar
            def _(eng):
                run("act", eng)

            @block.vector
            def _(eng):
                run("dve", eng)

            @block.gpsimd
            def _(eng):
                run("pool", eng)

            @block.sync
            def _(eng):
                run("sp", eng)

        return nc


def MM(out, lhsT, rhs, start=True, stop=True):
    return lambda e: e.matmul(out, lhsT=lhsT, rhs=rhs, start=start, stop=stop)


def TR(out, in_, ident):
    return lambda e: e.transpose(out, in_, ident)


def ACT(out, in_, func, **kw):
    return lambda e: e.activation(out=out, in_=in_, func=func, **kw)


def TT(out, a, b, op):
    return lambda e: e.tensor_tensor(out=out, in0=a, in1=b, op=op)


def TS(out, a, s1, op0, s2=None, op1=None):
    if op1 is None:
        return lambda e: e.tensor_scalar(out=out, in0=a, scalar1=s1, scalar2=None, op0=op0)
    return lambda e: e.tensor_scalar(out=out, in0=a, scalar1=s1, scalar2=s2, op0=op0, op1=op1)


def STT(out, a, s, b, op0, op1):
    return lambda e: e.scalar_tensor_tensor(out=out, in0=a, scalar=s, in1=b, op0=op0, op1=op1)


def CP(out, in_):
    return lambda e: e.tensor_copy(out=out, in_=in_)


def MS(out, v):
    return lambda e: e.memset(out, v)


def RCP(out, in_):
    return lambda e: e.reciprocal(out=out, in_=in_)


def SCAN(out, d0, d1):
    return lambda e: e.tensor_tensor_scan(out=out, data0=d0, data1=d1, initial=0.0, op0=ALU.mult, op1=ALU.add)


PV_G = 0
PV_MU = 8
PV_W0 = 22
PV_A0 = 26
PV_KK = 30
PV_KA = 34
PV_RK = 38
PV_LW = 42
PV_LB = 46
NPV = 64

RW0 = 1536
WCOLS = RW0 + 1792
PV_QG = 50
PV_KSG = 51
PV_KWG = 52
PV_KCG = 53
PV_G2 = 54
SLOPES = [2.0 ** -(i + 1) for i in range(8)]
NEGB = -1.0e9


def build(stage="all", n_hp=4, n_tt=NT, n_att=NT):
    P = Prog()
    nc = P.nc
    xT = nc.dram_tensor("xT", [8, 128, T], F32, kind="ExternalInput").ap()
    w_in = nc.dram_tensor("w_in", [8, 128, WCOLS], F32, kind="ExternalInput").ap()
    pvec = nc.dram_tensor("pvec", [128, NPV], F32, kind="ExternalInput").ap()
    w2a2 = nc.dram_tensor("w2a2", [128, 512], F32, kind="ExternalInput").ap()
    g2d = nc.dram_tensor("g2", [128, 512], F32, kind="ExternalInput").ap()
    cst = nc.dram_tensor("cst", [128, 7 * 128], F32, kind="ExternalInput").ap()
    mixT = nc.dram_tensor("mixT", [8, 128, T], BF16, kind=("ExternalOutput" if stage != "all" else "Internal")).ap()

    B = [nc.alloc_psum_tensor(f"bank{i}", [128, 512], F32) for i in range(6)]
    B6 = nc.alloc_psum_tensor("bank6", [128, 1024], BF16)
    B7 = nc.alloc_psum_tensor("bank7", [128, 512], F32)

    pv = P.sb([128, NPV], F32)
    cf = P.sb([128, 7 * 128], F32)
    identb = P.sb([128, 128], BF16)
    BOb = P.sb([128, 128], BF16)
    onesb = P.sb([128, 128], BF16)
    ones32 = P.sb([128, 128], F32)
    omka = P.sb([128, 4], F32)
    w2a2b = P.sb([128, 512], BF16)
    g2b = P.sb([128, 512], BF16)
    P.dma("sp", pv[:], pvec, writes=["pv"])
    P.dma("sp", cf[:], cst, writes=["cf"])
    P.op("dve", CP(identb[:], cf[:, 640:768]), reads=["cf"], writes=["identb"])
    P.op("dve", CP(BOb[:], cf[:, 768:896]), reads=["cf"], writes=["BOb"])
    P.op("pool", MS(ones32[:], 1.0), writes=["ones32"])
    P.op("pool", MS(onesb[:], 1.0), writes=["onesb"])
    P.op("dve", TS(omka[:], pv[:, PV_KA:PV_KA + 4], -1.0, ALU.mult, 1.0, ALU.add), reads=["pv"], writes=["omka"])
    MASK4 = cf[:, 0:512]
    MIU = cf[:, 512:640]

    XN_OFF = 229312 - 8 * T * 2
    xn = P.sb_at([128, 8, T], BF16, XN_OFF)
    P.sb_limit = XN_OFF

    P.mark()
    stg = [P.sb([128, 512], F32) for _ in range(2)]
    P.dma("sp", stg[0][:], w2a2, writes=["stg0"])
    P.dma("sp", stg[1][:], g2d, writes=["stg1"])
    P.op("pool", CP(w2a2b[:], stg[0][:]), reads=["stg0"], writes=["w2a2b"])
    P.op("pool", CP(g2b[:], stg[1][:]), reads=["stg1"], writes=["g2b"])
    xin = [P.sb([128, 8, 512], F32) for _ in range(2)]
    sq = P.sb([128, 8, 512], BF16)
    sd = P.sb([128, 512], F32)
    rstd = P.sb([128, 512], F32)
    for tt in range(NT):
        ts = slice(tt * 512, (tt + 1) * 512)
        xi = xin[tt % 2]
        kx = f"xin{tt % 2}"
        P.dma("sp", xi[:], xT[:, :, ts].rearrange("c p t -> p c t"), writes=[kx])
        P.op("act", ACT(sq[:], xi[:], AF.Square), reads=[kx], writes=["sq"])
        for dc in range(8):
            P.op("pe", MM(B[0][:], onesb[:], sq[:, dc, :], start=dc == 0, stop=dc == 7),
                 reads=["sq", "onesb"], writes=["B0"])
        P.op("act", ACT(sd[:], B[0][:], AF.Ln, scale=1.0 / 1024, bias=1e-6), reads=["B0"], writes=["sd"])
        P.op("act", ACT(rstd[:], sd[:], AF.Exp, scale=-0.5), reads=["sd"], writes=["rstd"])
        for dc in range(8):
            P.op("dve", STT(xn[:, dc, ts], xi[:, dc, :], pv[:, PV_G + dc:PV_G + dc + 1], rstd[:], ALU.mult, ALU.mult),
                 reads=[kx, "pv", "rstd"], writes=["xn"])
    P.barrier()
    P.release()
    if stage == "1a":
        dbg = nc.dram_tensor("dbg", [128, 8, T], BF16, kind="ExternalOutput").ap()
        P.dma("sp", dbg, xn[:], reads=["xn"])
        return P

    P.phase('1B')
    P.mark()
    Wr = P.sb([128, 8, 1792], BF16)
    P.mark()
    wst = [P.sb([128, 1792], F32) for _ in range(2)]
    for dc in range(8):
        k = f"wst{dc % 2}"
        P.dma("sp", wst[dc % 2][:], w_in[dc, :, RW0:RW0 + 1792], writes=[k])
        P.op("pool" if dc % 2 == 0 else "dve", CP(Wr[:, dc, :], wst[dc % 2][:]), reads=[k], writes=["Wr"])
    P.barrier()
    P.release()

    _rb = (P.sb_off + 63) // 64 * 64
    cbs = [P.sb_at([128, 513], F32, _rb + i_ * 2112) for i_ in range(2)]
    S1r = [[P.sb_at([128, 512], BF16, _rb + (2 * g_ + h_) * 1024) for h_ in range(2)] for g_ in range(2)]
    ARKr = [[P.sb_at([128, 128], BF16, _rb + 4096 + (2 * g_ + h_) * 256) for h_ in range(2)] for g_ in range(2)]
    P.sb_off = _rb + 5120
    P.sb_peak = max(P.sb_peak, P.sb_off)
    cb_ctr = [0]
    ip_ctr = [0]
    tnames = ["sg", "al", "cs", "dm", "E1", "E2", "E3", "kk", "kp", "t1"]
    tm_alias = {"ka": "sg", "t2": "t1"}
    tm = {n: P.sb([128, 512], F32) for n in tnames}
    for a_, b_ in tm_alias.items():
        tm[a_] = tm[b_]
    kk2 = P.sb([128, 512], BF16)
    dtmp = tm["t1"]

    def inproj(wt, wkey, wcol, dst, dkey, mu_col, post=None):
        for tt in range(NT):
            ts = slice(tt * 512, (tt + 1) * 512)
            bi = ip_ctr[0] % 2
            ip_ctr[0] += 1
            bank, bk = B[bi], f"B{bi}"
            for dc in range(8):
                P.op("pe", MM(bank[:], wt[:, dc, wcol:wcol + 128], xn[:, dc, ts], start=dc == 0, stop=dc == 7),
                     reads=[wkey, "xn"], writes=[bk])
            if mu_col is None:
                P.op("act", ACT(dst[:, ts], bank[:], AF.Copy), reads=[bk], writes=[dkey])
            else:
                i = cb_ctr[0]
                cb_ctr[0] += 1
                cur, prev = cbs[i % 2], cbs[(i - 1) % 2]
                kc, kp = f"cb{i % 2}", f"cb{(i - 1) % 2}"
                if tt == 0:
                    P.op("pool", MS(cur[:, 0:1], 0.0), writes=[kc])
                else:
                    P.op("pool", CP(cur[:, 0:1], prev[:, 512:513]), reads=[kp], writes=[kc])
                P.op("act", ACT(cur[:, 1:513], bank[:], AF.Copy), reads=[bk], writes=[kc])
                P.op("dve", TT(dtmp[:], cur[:, 0:512], cur[:, 1:513], ALU.subtract), reads=[kc], writes=["t1"])
                P.op("dve", STT(dst[:, ts], dtmp[:], pv[:, mu_col:mu_col + 1], cur[:, 1:513], ALU.mult, ALU.add),
                     reads=["t1", kc, "pv"], writes=[dkey])
            if post is not None:
                post(tt, ts)

    TWXA = P.sb([128, T], BF16)
    SG = P.sb([128, T], BF16)
    inproj(Wr, "Wr", 1536, TWXA, "TWXA", PV_MU + 12,
           post=lambda tt, ts: P.op("act", ACT(TWXA[0:64, ts], TWXA[0:64, ts], AF.Tanh), reads=["TWXA"], writes=["TWXA"]))
    inproj(Wr, "Wr", 1664, SG, "SG", PV_MU + 13,
           post=lambda tt, ts: P.op("act", ACT(SG[:, ts], SG[:, ts], AF.Sigmoid), reads=["SG"], writes=["SG"]))

    if stage == "1b_tw":
        dbg = nc.dram_tensor("dbg", [2, 128, T], BF16, kind="ExternalOutput").ap()
        P.dma("sp", dbg[0], TWXA[:], reads=["TWXA"])
        P.dma("sp", dbg[1], SG[:], reads=["SG"])
        return P
    Zr = P.sb([128, T], BF16)
    Zk = P.sb([128, T], BF16)
    Zv = P.sb([128, T], BF16)
    rkr = P.sb([128, 512], BF16)
    ncC = P.sb([128, 4], F32)
    gC = P.sb([128, 32], F32)
    opn = ["At", "Bt", "Kt", "Rt", "BGt", "KGt"]
    ops_ = [{n: P.sb([128, 512], BF16) for n in opn} for _ in range(2)]
    BON = [P.sb([128, 512], F32) for _ in range(2)]
    YT = [P.sb([128, 512], F32) for _ in range(2)]
    S1 = [[P.sb([128, 512], BF16) for _ in range(2)] for _ in range(2)]
    ARK = [[P.sb([128, 128], BF16) for _ in range(2)] for _ in range(2)]
    TOK = [P.sb([128, 512], BF16) for _ in range(2)]
    X32 = P.sb([128, 2, 256], F32)
    Xbf = P.sb([128, 2, 256], BF16)
    LTn = [[P.sb([128, 512], BF16) for _ in range(2)] for _ in range(2)]
    WTs = P.sb([128, 2, 128], BF16)
    Ubf = [P.sb([128, 128], BF16) for _ in range(2)]
    Upad = [[P.sb([128, 128], BF16) for _ in range(2)] for _ in range(2)]
    Vpad = [[P.sb([128, 128], BF16) for _ in range(2)] for _ in range(2)]
    for a_ in range(2):
        for b_ in range(2):
            P.op("pool", MS(Upad[a_][b_][:], 0.0), writes=[f"Upad{a_}"])
            P.op("pool", MS(Vpad[a_][b_][:], 0.0), writes=[f"Vpad{a_}"])
    M32 = P.sb([128, 128], F32)
    Mb2 = [P.sb([128, 128], BF16) for _ in range(2)]
    ybf = kk2
    ym = tm["E3"]
    MO = [P.sb([128, 512], BF16) for _ in range(1)]

    def prep(hp, tt, s):
        ts = slice(tt * 512, (tt + 1) * 512)
        hc = slice(hp * 128, (hp + 1) * 128)
        O = ops_[s]
        ok = lambda n: f"{n}{s}"
        col = lambda base: pv[:, base + hp:base + hp + 1]
        P.op("pe", MM(B[0][:], w2a2b[0:64, hc], TWXA[0:64, ts]), reads=["w2a2b", "TWXA"], writes=["B0"])
        P.op("pe", MM(B[1][:], w2a2b[64:128, hc], TWXA[64:128, ts]), reads=["w2a2b", "TWXA"], writes=["B1"])
        P.op("act", ACT(tm["sg"][:], B[0][:], AF.Sigmoid, bias=col(PV_W0)), reads=["B0", "pv"], writes=["sg"])
        P.op("act", ACT(tm["al"][:], B[1][:], AF.Sigmoid, bias=col(PV_A0)), reads=["B1", "pv"], writes=["al"])
        for c in range(4):
            c_ = slice(c * 128, (c + 1) * 128)
            P.op("dve", SCAN(tm["cs"][:, c_], ones32[:], tm["sg"][:, c_]), reads=["sg", "ones32"], writes=["cs"])
        P.op("dve", TS(ncC[:], tm["cs"][:, 127::128], -DEC, ALU.mult), reads=["cs"], writes=["ncC"])
        P.op("act", ACT(gC[:, tt * 4:(tt + 1) * 4], ncC[:], AF.Exp), reads=["ncC"], writes=["gC"])
        P.op("act", ACT(tm["E1"][:], tm["cs"][:], AF.Exp, scale=-DEC), reads=["cs"], writes=["E1"])
        P.op("act", ACT(tm["E2"][:], tm["cs"][:], AF.Exp, scale=DEC), reads=["cs"], writes=["E2"])
        P.op("dve", TT(tm["dm"][:], tm["cs"][:], tm["sg"][:], ALU.subtract), reads=["cs", "sg"], writes=["dm"])
        P.op("act", ACT(tm["dm"][:], tm["dm"][:], AF.Exp, scale=-DEC), reads=["dm"], writes=["dm"])
        for c in range(4):
            c_ = slice(c * 128, (c + 1) * 128)
            P.op("act", ACT(tm["E3"][:, c_], tm["cs"][:, c_], AF.Exp, scale=DEC, bias=ncC[:, c:c + 1]),
                 reads=["cs", "ncC"], writes=["E3"])
        P.op("act", ACT(tm["kk"][:], Zk[:, ts], AF.Copy, scale=col(PV_KK)), reads=["Zk", "pv"], writes=["kk"])
        P.op("act", ACT(kk2[:], tm["kk"][:], AF.Square), reads=["kk"], writes=["kk2"])
        P.op("pe", MM(B[0][:], BOb[:], kk2[:]), reads=["BOb", "kk2"], writes=["B0"])
        P.op("dve", TS(tm["t1"][:], B[0][:], 1e-18, ALU.max), reads=["B0"], writes=["t1"])
        P.op("act", ACT(tm["t1"][:], tm["t1"][:], AF.Ln), reads=["t1"], writes=["t1"])
        P.op("act", ACT(tm["t1"][:], tm["t1"][:], AF.Exp, scale=-0.5), reads=["t1"], writes=["t1"])
        P.op("dve", TT(tm["kk"][:], tm["kk"][:], tm["t1"][:], ALU.mult), reads=["kk", "t1"], writes=["kk"])
        P.op("act", ACT(tm["t2"][:], tm["al"][:], AF.Identity, scale=col(PV_KA), bias=omka[:, hp:hp + 1]),
             reads=["al", "pv", "omka"], writes=["t1"])
        P.op("pool", TT(tm["kp"][:], Zk[:, ts], tm["t2"][:], ALU.mult), reads=["Zk", "t1"], writes=["kp"])
        P.op("dve", STT(rkr[:], tm["kp"][:], col(PV_RK), Zr[:, ts], ALU.mult, ALU.mult), reads=["kp", "pv", "Zr"], writes=["rkr"])
        P.op("pe", MM(B[1][:], BOb[:], rkr[:]), reads=["BOb", "rkr"], writes=["B1"])
        P.op("dve", TT(BON[s][:], B[1][:], Zv[:, ts], ALU.mult), reads=["B1", "Zv"], writes=[ok("BON")])
        P.op("dve", STT(O["At"][:], tm["kk"][:], -1.0, tm["dm"][:], ALU.mult, ALU.mult), reads=["kk", "dm"], writes=[ok("At")])
        P.op("pool", TT(tm["ka"][:], tm["kk"][:], tm["al"][:], ALU.mult), reads=["kk", "al"], writes=["sg"])
        P.op("dve", TT(O["Bt"][:], tm["ka"][:], tm["E2"][:], ALU.mult), reads=["sg", "E2"], writes=[ok("Bt")])
        P.op("pool", TT(O["BGt"][:], tm["ka"][:], tm["E3"][:], ALU.mult), reads=["sg", "E3"], writes=[ok("BGt")])
        P.op("pool", TT(O["Kt"][:], tm["kp"][:], tm["E2"][:], ALU.mult), reads=["kp", "E2"], writes=[ok("Kt")])
        P.op("pool", TT(O["KGt"][:], tm["kp"][:], tm["E3"][:], ALU.mult), reads=["kp", "E3"], writes=[ok("KGt")])
        P.op("dve", TT(O["Rt"][:], Zr[:, ts], tm["E1"][:], ALU.mult), reads=["Zr", "E1"], writes=[ok("Rt")])

    def group(hp, tt, gi, s):
        O = ops_[s]
        ok = lambda n: f"{n}{s}"
        opk = [ok(n) for n in opn]
        G = 2
        ccs = [2 * gi + g for g in range(G)]
        c_s = [slice(cc * 128, (cc + 1) * 128) for cc in ccs]
        gcs = [slice(tt * 512 + cc * 128, tt * 512 + (cc + 1) * 128) for cc in ccs]
        for g in range(G):
            c_ = c_s[g]
            for h in range(2):
                hs = slice(64 * h, 64 * h + 64)
                bank, bk = B[2 + 2 * g + h], f"B{2 + 2 * g + h}"
                P.op("pe", MM(bank[:, 0:128], O["At"][hs, c_], O["Bt"][hs, c_]), reads=opk, writes=[bk])
                P.op("pe", MM(bank[:, 128:256], O["Bt"][hs, c_], O["At"][hs, c_]), reads=opk, writes=[bk])
                P.op("pe", MM(bank[:, 256:384], O["Kt"][hs, c_], O["At"][hs, c_]), reads=opk, writes=[bk])
                P.op("pe", MM(bank[:, 384:512], O["Bt"][hs, c_], O["Rt"][hs, c_]), reads=opk, writes=[bk])
                P.op("dve", TT(S1[g][h][:], bank[:], MASK4, ALU.mult), reads=[bk, "cf"], writes=[f"S1_{g}{h}"])
        if stage == 'g1':
            return True
        for g in range(G):
            c_ = c_s[g]
            o6 = g * 512
            P.op("pe", TR(B6[:, o6:o6 + 128], O["At"][:, c_], identb[:]), reads=opk + ["identb"], writes=["B6"])
            P.op("pe", TR(B6[:, o6 + 128:o6 + 256], Zv[:, gcs[g]], identb[:]), reads=["Zv", "identb"], writes=["B6"])
            P.op("pe", TR(B6[:, o6 + 256:o6 + 384], O["BGt"][:, c_], identb[:]), reads=opk + ["identb"], writes=["B6"])
            P.op("pe", TR(B6[:, o6 + 384:o6 + 512], O["KGt"][:, c_], identb[:]), reads=opk + ["identb"], writes=["B6"])
        for g in range(G):
            P.op("act", ACT(TOK[g][:], B6[:, g * 512:(g + 1) * 512], AF.Copy), reads=["B6"], writes=[f"TOK{g}"])
            for h in range(2):
                P.op("pool", CP(Vpad[g][h][:, 64 * h:64 * h + 64], TOK[g][:, 128 + 64 * h:128 + 64 * h + 64]), reads=[f"TOK{g}"], writes=[f"Vpad{g}"])
        if stage == 'g2':
            return True
        for g in range(G):
            for h in range(2):
                v_ = slice(h * 64, (h + 1) * 64)
                P.op("pe", MM(B7[:, g * 128 + h * 64:g * 128 + (h + 1) * 64], S1[g][h][:, 256:384], TOK[g][:, 128 + h * 64:128 + (h + 1) * 64]),
                     reads=[f"S1_{g}{h}", f"TOK{g}"], writes=["B7"])
        for g in range(G):
            P.op("act", ACT(X32[:, g, 0:128], TOK[g][:, 0:128], AF.Copy), reads=[f"TOK{g}"], writes=[f"X32{g}"])
            P.op("act", ACT(Xbf[:, g, 0:128], TOK[g][:, 0:128], AF.Copy), reads=[f"TOK{g}"], writes=[f"Xbf{g}"])
        b7v = B7[:, 0:256].rearrange("p (g c) -> p g c", c=128)
        P.op("act", ACT(X32[:, :, 128:256], b7v, AF.Copy), reads=["B7"], writes=["X320", "X321"])
        P.op("dve", CP(Xbf[:, :, 128:256], b7v), reads=["B7"], writes=["Xbf0", "Xbf1"])
        if stage == 'g3':
            return True
        X32f = X32[:, :, :].rearrange("p g c -> p (g c)")
        Xbff = Xbf[:, :, :].rearrange("p g c -> p (g c)")
        for j in range(7):
            Ls, Ts, lks = [], [], []
            for g in range(G):
                if j == 0:
                    Ls.append([S1[g][h][:, 0:128] for h in range(2)])
                    Ts.append([S1[g][h][:, 128:256] for h in range(2)])
                    lks.append([f"S1_{g}0", f"S1_{g}1"])
                else:
                    lt = LTn[g][j % 2]
                    Ls.append([lt[:, h * 256:h * 256 + 128] for h in range(2)])
                    Ts.append([lt[:, h * 256 + 128:h * 256 + 256] for h in range(2)])
                    lks.append([f"LTn{g}{j % 2}"])
            for g in range(G):
                if j < 6:
                    bank, bk = B[2 + g], f"B{2 + g}"
                    for h in range(2):
                        P.op("pe", MM(bank[:, h * 256:h * 256 + 128], Ts[g][h], Ls[g][h]), reads=lks[g], writes=[bk])
                        P.op("pe", MM(bank[:, h * 256 + 128:h * 256 + 256], Ls[g][h], Ts[g][h]), reads=lks[g], writes=[bk])
                    P.op("act", ACT(LTn[g][(j + 1) % 2][:], bank[:], AF.Copy), reads=[bk], writes=[f"LTn{g}{(j + 1) % 2}"])
                ab, abk = (B[4], "B4") if g == 0 else (B[5], "B5")
                for h in range(2):
                    for part in range(2):
                        q0 = part * 128 + h * 64
                        P.op("pe", MM(ab[:, q0:q0 + 64], Ts[g][h], Xbf[:, g, q0:q0 + 64]), reads=lks[g] + [f"Xbf{g}"], writes=[abk])
                if j < 6:
                    P.op("dve", TT(Xbf[:, g, :], ab[:, 0:256], X32[:, g, :], ALU.add), reads=[abk, f"X32{g}"], writes=[f"Xbf{g}"])
                else:
                    P.op("dve", TT(Xbf[:, g, 0:128], ab[:, 0:128], X32[:, g, 0:128], ALU.add), reads=[abk, f"X32{g}"], writes=[f"Xbf{g}"])
                P.op("dve", TT(X32[:, g, :], ab[:, 0:256], X32[:, g, :], ALU.add), reads=[abk, f"X32{g}"], writes=[f"X32{g}"])
            bg_step()
        if stage == 'g4':
            return True
        for g in range(G):
            c_ = c_s[g]
            for h in range(2):
                hs = slice(64 * h, 64 * h + 64)
                b2, b2k = (B[5], "B5") if h == 0 else (B[4], "B4")
                o2 = (256 if h == 0 else 0) + g * 128
                P.op("pe", MM(b2[:, o2:o2 + 128], O["Kt"][hs, c_], O["Rt"][hs, c_]), reads=opk, writes=[b2k])
                P.op("dve", TT(ARK[g][h][:], b2[:, o2:o2 + 128], MIU, ALU.mult), reads=[b2k, "cf"], writes=[f"ARK_{g}{h}"])
        for g in range(G):
            P.op("pe", TR(B6[:, g * 128:(g + 1) * 128], Xbf[:, g, 0:128], identb[:]), reads=[f"Xbf{g}", "identb"], writes=["B6"])
        P.op("act", ACT(WTs[:, :, :].rearrange("p g c -> p (g c)"), B6[:, 0:256], AF.Copy), reads=["B6"], writes=["WTs"])
        if stage == 'g5':
            return True
        for g in range(G):
            c_ = c_s[g]
            ci = tt * 4 + ccs[g]
            uk = f"Ubf{g}"
            Mbf, mbk = Mb2[ci % 2], f"Mb{ci % 2}"
            Mnx, mnk = Mb2[(ci + 1) % 2], f"Mb{(ci + 1) % 2}"
            P.op("pe", MM(B[5][:, 0:128], WTs[:, g, :], Mbf[:]), reads=["WTs", mbk], writes=["B5"])
            for h in range(2):
                hc_ = slice(64 * h, 64 * h + 64)
                P.op("dve", TT(Upad[g][h][:, hc_], B[5][:, hc_], X32[:, g, 128 + 64 * h:128 + 64 * h + 64], ALU.add),
                     reads=["B5", f"X32{g}"], writes=[f"Upad{g}"])
            psM = B[5][:, 128:256]
            P.op("pe", MM(psM, TOK[g][:, 256:384], Upad[g][0][:], start=True, stop=False), reads=[f"TOK{g}", f"Upad{g}"], writes=["B5"])
            P.op("pe", MM(psM, TOK[g][:, 256:384], Upad[g][1][:], start=False, stop=False), reads=[f"TOK{g}", f"Upad{g}"], writes=["B5"])
            P.op("pe", MM(psM, TOK[g][:, 384:512], TOK[g][:, 128:256], start=False, stop=True), reads=[f"TOK{g}"], writes=["B5"])
            psY = B7[:, 256 + g * 128:256 + (g + 1) * 128]
            P.op("pe", MM(psY, Mbf[:], O["Rt"][:, c_], start=True, stop=False), reads=[mbk] + opk, writes=["B7"])
            for h in range(2):
                P.op("pe", MM(psY, Upad[g][h][:], S1[g][h][:, 384:512], start=False, stop=False), reads=[f"Upad{g}", f"S1_{g}{h}"], writes=["B7"])
                P.op("pe", MM(psY, Vpad[g][h][:], ARK[g][h][:], start=False, stop=(h == 1)), reads=[f"Vpad{g}", f"ARK_{g}{h}"], writes=["B7"])
            for h in range(2):
                hs = slice(64 * h, 64 * h + 64)
                P.op("dve", STT(Mnx[hs, hs], M32[hs, hs], gC[hs, ci:ci + 1], B[5][hs, 128 + 64 * h:128 + 64 * h + 64], ALU.mult, ALU.add),
                     reads=["M32", "gC", "B5"], writes=[mnk])
            for h in range(2):
                hs = slice(64 * h, 64 * h + 64)
                P.op("dve", STT(M32[hs, hs], M32[hs, hs], gC[hs, ci:ci + 1], B[5][hs, 128 + 64 * h:128 + 64 * h + 64], ALU.mult, ALU.add),
                     reads=["M32", "gC", "B5"], writes=["M32"])
            P.op("act", ACT(YT[s][:, c_], psY, AF.Copy), reads=["B7"], writes=[f"YT{s}"])
            bg_step()

    def post(hp, tt, s):
        ts = slice(tt * 512, (tt + 1) * 512)
        hc = slice(hp * 128, (hp + 1) * 128)
        col = lambda base: pv[:, base + hp:base + hp + 1]
        yk = f"YT{s}"
        P.op("act", ACT(ybf[:], YT[s][:], AF.Copy), reads=[yk], writes=["kk2"])
        P.op("pe", MM(B[0][:], BOb[:], ybf[:]), reads=["BOb", "kk2"], writes=["B0"])
        P.op("dve", STT(ym[:], B[0][:], -1.0 / 64, YT[s][:], ALU.mult, ALU.add), reads=["B0", yk], writes=["E3"])
        P.op("act", ACT(ybf[:], ym[:], AF.Square), reads=["E3"], writes=["kk2"])
        P.op("pe", MM(B[1][:], BOb[:], ybf[:]), reads=["BOb", "kk2"], writes=["B1"])
        P.op("act", ACT(tm["t1"][:], B[1][:], AF.Ln, scale=1.0 / 64, bias=64e-5), reads=["B1"], writes=["t1"])
        P.op("act", ACT(tm["t1"][:], tm["t1"][:], AF.Exp, scale=-0.5), reads=["t1"], writes=["t1"])
        P.op("dve", TT(ym[:], ym[:], tm["t1"][:], ALU.mult), reads=["E3", "t1"], writes=["E3"])
        P.op("act", ACT(ym[:], ym[:], AF.Identity, scale=col(PV_LW), bias=col(PV_LB)), reads=["E3", "pv"], writes=["E3"])
        P.op("pool", TT(ym[:], ym[:], BON[s][:], ALU.add), reads=["E3", f"BON{s}"], writes=["E3"])
        P.op("pe", MM(B[0][:], g2b[:, hc], SG[:, ts]), reads=["g2b", "SG"], writes=["B0"])
        mk = "MO0"
        P.op("dve", TT(MO[0][:], ym[:], B[0][:], ALU.mult), reads=["E3", "B0"], writes=[mk])
        P.dma("sp", mixT[4 + hp, :, ts], MO[0][:], reads=[mk], writes=[f"mixT{4 + hp}_{tt}"])

    cidx = 0
    for hp in range(n_hp):
        P.phase(f'rwkv_hp{hp}')
        inproj(Wr, "Wr", 0 + hp * 128, Zr, "Zr", PV_MU + hp)
        inproj(Wr, "Wr", 512 + hp * 128, Zk, "Zk", PV_MU + 4 + hp)
        inproj(Wr, "Wr", 1024 + hp * 128, Zv, "Zv", PV_MU + 8 + hp)
        P.op("pool", MS(M32[:], 0.0), writes=["M32"])
        for q_ in range(2):
            P.op("pool", MS(Mb2[q_][:], 0.0), writes=[f"Mb{q_}"])
        bg = []
        BG_K = [4]

        def bg_step():
            P.replay(bg, BG_K[0])

        prep(hp, 0, 0)
        for tt in range(n_tt):
            s = tt % 2
            P.defer_begin()
            if tt > 0:
                post(hp, tt - 1, (tt - 1) % 2)
            if tt + 1 < n_tt:
                prep(hp, tt + 1, (tt + 1) % 2)
            bg.extend(P.defer_end())
            BG_K[0] = max(1, -(-len(bg) // 16))
            for gi in range(2):
                group(hp, tt, gi, s)
            P.replay(bg, len(bg))
        post(hp, n_tt - 1, (n_tt - 1) % 2)
    P.barrier()
    P.release()
    if stage == "rwkv":
        return P
    P.phase('nsa_proj')
    w_gl = nc.dram_tensor("w_gl", [128, 8, 24], F32, kind="ExternalInput").ap()
    gbd = nc.dram_tensor("gb", [128, 24], F32, kind="ExternalInput").ap()
    w1kd = nc.dram_tensor("w1k", [128, 32, 128], F32, kind="ExternalInput").ap()
    w1vd = nc.dram_tensor("w1v", [128, 32, 128], F32, kind="ExternalInput").ap()
    poskd = nc.dram_tensor("posk", [128, 32], F32, kind="ExternalInput").ap()
    posvd = nc.dram_tensor("posv", [128, 32], F32, kind="ExternalInput").ap()
    w2kd = nc.dram_tensor("w2k", [128, 128], F32, kind="ExternalInput").ap()
    w2vd = nc.dram_tensor("w2v", [128, 64], F32, kind="ExternalInput").ap()
    ovlad = nc.dram_tensor("ovla", [128, 2, 65], F32, kind="ExternalInput").ap()
    tzd = nc.dram_tensor("tz", [128, 4480], F32, kind="ExternalInput").ap()
    tzwd = nc.dram_tensor("tzw", [128, 1408], F32, kind="ExternalInput").ap()
    d0d = nc.dram_tensor("d0", [128, 4096], F32, kind="ExternalInput").ap()
    expd = nc.dram_tensor("expand", [128, 4096], F32, kind="ExternalInput").ap()
    kbabd = nc.dram_tensor("kbab", [128, 256], F32, kind="ExternalInput").ap()

    P.mark()
    QT = P.sb([128, 4, T], BF16)
    KSd = [P.sb([128, T], BF16) for _ in range(2)]
    KWd = [P.sb([128, T], BF16) for _ in range(2)]
    VSa = P.sb([128, 32, 2, 65], BF16)
    VWa = P.sb([128, 32, 2, 65], BF16)
    KCCd = [P.sb([128, 256], BF16) for _ in range(2)]
    VCa = P.sb([128, 2, 2, 130], BF16)
    GATES = P.sb([128, 32, 24], F32)
    qg8 = P.sb([128, 1], F32)
    P.op("dve", TS(qg8[:], pv[:, PV_QG:PV_QG + 1], 0.125, ALU.mult), reads=["pv"], writes=["qg8"])
    P.op("pool", MS(VSa[:], 1.0), writes=["VSa"])
    P.op("pool", MS(VWa[:], 1.0), writes=["VWa"])
    for kv in range(2):
        P.op("pool", MS(KCCd[kv][:], 0.0), writes=[f"KCCd{kv}"])

    P.mark()
    wst2 = [P.sb([128, 8, 128], F32) for _ in range(2)]
    wch = [P.sb([128, 8, 128], BF16) for _ in range(2)]
    vtmp = P.sb([128, T], BF16)
    nt32s = [P.sb([128, 512], F32) for _ in range(2)]
    nsqs = [P.sb([128, 512], BF16) for _ in range(2)]
    nsds = [P.sb([128, 512], F32) for _ in range(2)]
    nt32, nsq, nsd = nt32s[0], nsqs[0], nsds[0]
    wctr = [0]

    def load_wchunk(col0):
        i = wctr[0] % 2
        wctr[0] += 1
        P.dma("sp", wst2[i][:], w_in[:, :, col0:col0 + 128].rearrange("c p n -> p c n"), writes=[f"wst2{i}"])
        P.op("pool", CP(wch[i][:], wst2[i][:]), reads=[f"wst2{i}"], writes=[f"wch{i}"])
        return wch[i], f"wch{i}"

    pend2 = []

    def proj_chunk(col0, sink):
        wt, wk = load_wchunk(col0)
        for tt in range(NT):
            ts = slice(tt * 512, (tt + 1) * 512)
            bi = ip_ctr[0] % 2
            ip_ctr[0] += 1
            bank, bk = B[bi], f"B{bi}"
            for dc in range(8):
                P.op("pe", MM(bank[:], wt[:, dc, :], xn[:, dc, ts], start=dc == 0, stop=dc == 7), reads=[wk, "xn"], writes=[bk])
            while pend2:
                pend2.pop(0)()
            sink(tt, ts, bank, bk)

    def flush2():
        while pend2:
            pend2.pop(0)()

    nctr = [0]

    def normed_sink(dst_fn, dkey, gcol):
        def sink(tt, ts, bank, bk):
            u = nctr[0] % 2
            nctr[0] += 1
            a32, asq, asd = nt32s[u], nsqs[u], nsds[u]
            k32, ksq, ksd = f"nt32_{u}", f"nsq_{u}", f"nsd_{u}"
            b2_, b2k = B[2 + u], f"B{2 + u}"
            P.op("act", ACT(a32[:], bank[:], AF.Copy), reads=[bk], writes=[k32])
            P.op("act", ACT(asq[:], a32[:], AF.Square), reads=[k32], writes=[ksq])

            def second():
                P.op("pe", MM(b2_[:], BOb[:], asq[:]), reads=["BOb", ksq], writes=[b2k])
                P.op("act", ACT(asd[:], b2_[:], AF.Ln, scale=1.0 / 64, bias=1e-6), reads=[b2k], writes=[ksd])
                P.op("act", ACT(asd[:], asd[:], AF.Exp, scale=-0.5), reads=[ksd], writes=[ksd])
                P.op("dve", STT(dst_fn(ts), a32[:], gcol, asd[:], ALU.mult, ALU.mult), reads=[k32, ksd, "pv", "qg8"], writes=[dkey])
            pend2.append(second)
        return sink

    def copy_sink(tt, ts, bank, bk):
        P.op("act", ACT(vtmp[:, ts], bank[:], AF.Copy), reads=[bk], writes=["vtmp"])

    for c in range(4):
        proj_chunk(c * 128, normed_sink(lambda ts, c=c: QT[:, c, ts], "QT", qg8[:, 0:1]))
    for kv in range(2):
        proj_chunk(768 + 128 * kv, normed_sink(lambda ts, kv=kv: KSd[kv][:, ts], f"KSd{kv}", pv[:, PV_KSG:PV_KSG + 1]))
        proj_chunk(1152 + 128 * kv, normed_sink(lambda ts, kv=kv: KWd[kv][:, ts], f"KWd{kv}", pv[:, PV_KWG:PV_KWG + 1]))

    flush2()
    for col0, Va, vk in ((1024, VSa, "VSa"), (1408, VWa, "VWa")):
        proj_chunk(col0, copy_sink)
        for sc in range(32):
            q4 = sc % 4
            P.op("pe", TR(B6[:, q4 * 128:(q4 + 1) * 128], vtmp[:, sc * 128:(sc + 1) * 128], identb[:]), reads=["vtmp", "identb"], writes=["B6"])
            P.op("dve" if sc % 2 == 0 else "act",
                 CP(Va[:, sc, :, 0:64], B6[:, q4 * 128:(q4 + 1) * 128].rearrange("p (a b) -> p a b", b=64)) if sc % 2 == 0 else
                 ACT(Va[:, sc, :, 0:64], B6[:, q4 * 128:(q4 + 1) * 128].rearrange("p (a b) -> p a b", b=64), AF.Copy),
                 reads=["B6"], writes=[vk])

    wgl32 = P.sb([128, 8, 24], F32)
    wglb = P.sb([128, 8, 24], BF16)
    gbt = P.sb([128, 24], F32)
    P.dma("sp", wgl32[:], w_gl, writes=["wgl32"])
    P.dma("sp", gbt[:], gbd, writes=["gbt"])
    P.op("pool", CP(wglb[:], wgl32[:]), reads=["wgl32"], writes=["wglb"])
    for half in range(2):
        bank, bk = B[3 + half], f"B{3 + half}"
        for s16 in range(16):
            sub = half * 16 + s16
            for dc in range(8):
                P.op("pe", MM(bank[:, s16 * 24:(s16 + 1) * 24], xn[:, dc, sub * 128:(sub + 1) * 128], wglb[:, dc, :], start=dc == 0, stop=dc == 7),
                     reads=["xn", "wglb"], writes=[bk])
        for s16 in range(16):
            sub = half * 16 + s16
            P.op("dve", TT(GATES[:, sub, :], bank[:, s16 * 24:(s16 + 1) * 24], gbt[:], ALU.add), reads=[bk, "gbt"], writes=["GATES"])
    P.op("act", ACT(GATES[:], GATES[:], AF.Sigmoid), reads=["GATES"], writes=["GATES"])

    W1b = P.sb([128, 32, 128], BF16)
    posb = P.sb([128, 32], BF16)
    pos32 = P.sb([128, 32], F32)
    w2k32 = P.sb([128, 128], F32)
    w2kb = P.sb([128, 128], BF16)
    w2v32 = P.sb([128, 64], F32)
    w2vb = P.sb([128, 64], BF16)
    cbias = P.sb([128, 1], F32)
    GH = [P.sb([128, 256], BF16) for _ in range(2)]
    ovl32 = P.sb([128, 2, 65], F32)
    P.dma("sp", w2k32[:], w2kd, writes=["w2k32"])
    P.dma("sp", w2v32[:], w2vd, writes=["w2v32"])
    P.dma("sp", ovl32[:], ovlad, writes=["ovl32"])
    P.op("pool", CP(w2kb[:], w2k32[:]), reads=["w2k32"], writes=["w2kb"])
    P.op("pool", CP(w2vb[:], w2v32[:]), reads=["w2v32"], writes=["w2vb"])
    for kv in range(2):
        P.op("pool", MS(GH[kv][:], 0.0), writes=[f"GH{kv}"])
        for n2 in range(2):
            P.op("pool", CP(VCa[:, n2, kv, 64:129], ovl32[:, n2, :]), reads=["ovl32"], writes=["VCa"])

    for which, col0, w1d, posd in (("k", 512, w1kd, poskd), ("v", 640, w1vd, posvd)):
        proj_chunk(col0, copy_sink)
        for q in range(4):
            i = wctr[0] % 2
            wctr[0] += 1
            P.dma("sp", wst2[i][:], w1d[:, q * 8:(q + 1) * 8, :], writes=[f"wst2{i}"])
            P.op("pool", CP(W1b[:, q * 8:(q + 1) * 8, :], wst2[i][:]), reads=[f"wst2{i}"], writes=["W1b"])
        P.dma("sp", pos32[:], posd, writes=["pos32"])
        P.op("pool", CP(posb[:], pos32[:]), reads=["pos32"], writes=["posb"])
        for l in range(32):
            P.op("pe", MM(B[2][:, 0:1], W1b[0:64, l, :], posb[0:64, l:l + 1], start=l == 0, stop=l == 31), reads=["W1b", "posb"], writes=["B2"])
        P.op("dve", CP(cbias[:], B[2][:, 0:1]), reads=["B2"], writes=["cbias"])
        for kv in range(2):
            hs = slice(64 * kv, 64 * kv + 64)
            bank, bk = B[3 + kv], f"B{3 + kv}"
            for l in range(32):
                P.op("pe", MM(bank[:, 0:255], W1b[hs, l, :], vtmp[hs, l:l + 16 * 254 + 1:16], start=l == 0, stop=l == 31),
                     reads=["W1b", "vtmp"], writes=[bk])
            gx, gu = nt32[:, 0:255], nsd[:, 0:255]
            P.op("act", ACT(gx, bank[:, 0:255], AF.Identity, bias=cbias[:, 0:1]), reads=[bk, "cbias"], writes=["nt32_0"])
            P.op("act", ACT(gu, gx, AF.Square), reads=["nt32_0"], writes=["nsd_0"])
            P.op("dve", TS(gu, gu, 0.044715, ALU.mult, 1.0, ALU.add), reads=["nsd_0"], writes=["nsd_0"])
            P.op("dve", TT(gu, gu, gx, ALU.mult), reads=["nsd_0", "nt32_0"], writes=["nsd_0"])
            P.op("act", ACT(gu, gu, AF.Tanh, scale=0.7978845608028654), reads=["nsd_0"], writes=["nsd_0"])
            P.op("dve", STT(gu, gu, 1.0, gx, ALU.add, ALU.mult), reads=["nsd_0", "nt32_0"], writes=["nsd_0"])
            P.op("dve", TS(GH[kv][:, 0:255], gu, 0.5, ALU.mult), reads=["nsd_0"], writes=[f"GH{kv}"])
        for kv in range(2):
            if which == "k":
                P.op("pe", MM(B[2][:, 0:256], w2kb[:], GH[kv][:]), reads=["w2kb", f"GH{kv}"], writes=["B2"])
                P.op("act", ACT(nt32[:, 0:256], B[2][:, 0:256], AF.Copy), reads=["B2"], writes=["nt32_0"])
                P.op("act", ACT(nsq[:, 0:256], nt32[:, 0:256], AF.Square), reads=["nt32_0"], writes=["nsq_0"])
                P.op("pe", MM(B[2][:, 0:256], BOb[:], nsq[:, 0:256]), reads=["BOb", "nsq_0"], writes=["B2"])
                P.op("act", ACT(nsd[:, 0:256], B[2][:, 0:256], AF.Ln, scale=1.0 / 64, bias=1e-6), reads=["B2"], writes=["nsd_0"])
                P.op("act", ACT(nsd[:, 0:256], nsd[:, 0:256], AF.Exp, scale=-0.5), reads=["nsd_0"], writes=["nsd_0"])
                P.op("dve", STT(KCCd[kv][:, 0:255], nt32[:, 0:255], pv[:, PV_KCG:PV_KCG + 1], nsd[:, 0:255], ALU.mult, ALU.mult),
                     reads=["nt32_0", "nsd_0", "pv"], writes=[f"KCCd{kv}"])
            else:
                for n2 in range(2):
                    P.op("pe", MM(B[2][:, n2 * 64:(n2 + 1) * 64], GH[kv][:, n2 * 128:(n2 + 1) * 128], w2vb[:]), reads=["w2vb", f"GH{kv}"], writes=["B2"])
                for n2 in range(2):
                    P.op("dve", CP(VCa[:, n2, kv, 0:64], B[2][:, n2 * 64:(n2 + 1) * 64]), reads=["B2"], writes=["VCa"])
    P.barrier()
    P.release()
    P.sb_limit = 229312

    P.phase('nsa_att')
    Tz = P.sb([128, 4480], F32)
    TzW = P.sb([128, 1408], F32)
    D0 = P.sb([128, 4096], F32)
    kbab = P.sb([128, 256], F32)
    P.dma("sp", Tz[:], tzd, writes=["Tz"])
    P.dma("sp", TzW[:], tzwd, writes=["TzW"])
    P.dma("sp", kbab[:], kbabd, writes=["kbab"])
    LSE = [[KSd[kv], P.sb([128, T], BF16)] for kv in range(2)]
    P.mark()
    est = P.sb([128, 4096], F32)
    P.dma("sp", est[:], expd, writes=["est"])
    for kv in range(2):
        P.op("pool", CP(LSE[kv][1][64:128, :], KSd[kv][64:128, :]), reads=[f"KSd{kv}"], writes=[f"LSE{kv}1"])
        P.op("pool" if kv == 0 else "dve", CP(LSE[kv][1][0:64, :], est[0:64, :]), reads=["est"], writes=[f"LSE{kv}1"])
        P.op("dve" if kv == 0 else "pool", CP(KSd[kv][64:128, :], est[64:128, :]), reads=["est", f"LSE{kv}1"], writes=[f"KSd{kv}"])
    P.barrier()
    P.release()
    P.dma("sp", D0[:], d0d, writes=["D0"])

    NSB = 5
    SBK = [(B[0], "B0"), (B[1], "B1"), (B7, "B7"), (B[4], "B4"), (B[5], "B5")]
    PVB = [(B[2], "B2"), (B[3], "B3")]
    stmp = [P.sb([128, 512], F32) for _ in range(NSB)]
    PT = [P.sb([128, 512], BF16) for _ in range(NSB)]
    acc = P.sb([128, 4, 512], F32)
    accb = P.sb([128, 4, 512], BF16)
    imp = P.sb([128, 2, 4, 64], F32)
    etmp = P.sb([128, 4, 64], F32)
    iw = P.sb([128, 64], F32)
    iw2 = P.sb([128, 64], F32)
    m8 = P.sb([128, 16], F32)
    selb2 = P.sb([128, 128], BF16)
    SELBT = [P.sb([128, 512], BF16) for _ in range(2)]
    RS = [P.sb([128, 512], BF16) for _ in range(2)]
    rsc = [0]
    sm = [P.sb([128, 12], F32) for _ in range(2)]
    MXo = [P.sb([128, 512], BF16) for _ in range(2)]
    sctr = [0]
    pvc = [0]
    mxc = [0]
    ectr = [0]
    pend = []
    LOOK = 4

    def push(stage_a, stage_b):
        stage_a()
        pend.append(stage_b)
        if len(pend) > LOOK:
            pend.pop(0)()

    def flush():
        while pend:
            pend.pop(0)()

    def attend(tt, h, br, kT, kkey, Va, vkey, chunks, bias_ap, bkey, sub_range, nvc, sel=False, first_head=False):
        ts = slice(tt * 512, (tt + 1) * 512)
        kvh, qc = h // 4, h // 2
        qs = slice(64 * (h % 2), 64 * (h % 2) + 64)
        m_h = SLOPES[h]
        hcols = slice(h * 64, (h + 1) * 64)
        if nvc > 65:
            pb = [PVB[(pvc[0] + q) % len(PVB)] for q in range(2)]
            pvc[0] += 2
            reg = lambda j: (pb[j // 2][0][:, (j % 2) * 256:(j % 2) * 256 + nvc], pb[j // 2][1], pb[j // 2][0], (j % 2) * 256)
        else:
            pb = [PVB[pvc[0] % len(PVB)]]
            pvc[0] += 1
            reg = lambda j: (pb[0][0][:, j * 128:j * 128 + nvc], pb[0][1], pb[0][0], j * 128)
        if sel:
            ri = rsc[0] % 2
            rsc[0] += 1
            rk = f"RS{ri}"
            os_ = slice(64 * (1 - h % 2), 64 * (1 - h % 2) + 64)
            P.op("pool", CP(RS[ri][qs, :], QT[qs, qc, ts]), reads=["QT"], writes=[rk])
            P.op("pool", CP(RS[ri][os_, :], SELBT[kvh][os_, :]), reads=[f"SELBT{kvh}"], writes=[rk])
        started = set()
        pv_list = [(sc, j) for sc in chunks for j in range(4) if sub_range(j)[0] <= sc <= sub_range(j)[1]]
        last_for_bank = {}
        for sc, j in pv_list:
            last_for_bank[reg(j)[1]] = (sc, j)

        def make(sc, is_last):
            i = sctr[0] % NSB
            sctr[0] += 1
            bank, bk = SBK[i]

            def stage_a():
                if sel:
                    P.op("pe", MM(bank[:], LSE[kvh][h % 2][:, sc * 128:(sc + 1) * 128], RS[ri][:, :]),
                         reads=[f"KSd{kvh}", f"LSE{kvh}1", rk], writes=[bk])
                else:
                    P.op("pe", MM(bank[:], kT[qs, sc * 128:(sc + 1) * 128], QT[qs, qc, ts]), reads=[kkey, "QT"], writes=[bk])
                P.op("dve", STT(stmp[i][:], bias_ap(sc), m_h, bank[:], ALU.mult, ALU.add), reads=[bkey, bk], writes=[f"stmp{i}"])
                P.op("act", ACT(PT[i][:], stmp[i][:], AF.Exp), reads=[f"stmp{i}"], writes=[f"PT{i}"])

            def stage_b():
                for j in range(4):
                    lo, hi = sub_range(j)
                    if sc < lo or sc > hi:
                        continue
                    out_ap, pk, _, _ = reg(j)
                    P.op("pe", MM(out_ap, PT[i][:, j * 128:(j + 1) * 128], Va(sc, kvh), start=(pk not in started),
                                  stop=(last_for_bank[pk] == (sc, j))),
                         reads=[f"PT{i}", vkey], writes=[pk])
                    started.add(pk)
                if is_last:
                    epilogue()
            return stage_a, stage_b

        def epilogue():
            e = ectr[0] % 2
            ectr[0] += 1
            smk = f"sm{e}"
            S = sm[e]
            pks = sorted({reg(j)[1] for j in range(4)})
            if nvc > 65:
                for q in range(2):
                    v2 = pb[q][0][:, :].rearrange("p (j c) -> p j c", c=256)
                    P.op("dve", TS(S[:, 2 * q:2 * q + 2], v2[:, :, 64], 1e-30, ALU.max), reads=pks, writes=[smk])
            else:
                bv = pb[0][0][:, :].rearrange("p (j c) -> p j c", c=128)
                P.op("dve", TS(S[:, 0:4], bv[:, :, 64], 1e-30, ALU.max), reads=pks, writes=[smk])
            P.op("dve", RCP(S[:, 4:8], S[:, 0:4]), reads=[smk], writes=[smk])
            P.op("dve", TT(S[:, 8:12], S[:, 4:8], GATES[:, 4 * tt:4 * tt + 4, 3 * h + br], ALU.mult), reads=[smk, "GATES"], writes=[smk])
            if nvc > 65:
                for q in range(2):
                    v2 = pb[q][0][:, :].rearrange("p (j c) -> p j c", c=256)
                    pk = pb[q][1]
                    fb2 = S[:, 8 + 2 * q:10 + 2 * q].unsqueeze(2).to_broadcast([128, 2, 64])
                    rb2 = S[:, 4 + 2 * q:6 + 2 * q].unsqueeze(2).to_broadcast([128, 2, 64])
                    P.op("dve", TT(acc[:, 2 * q:2 * q + 2, hcols], v2[:, :, 0:64], fb2, ALU.mult), reads=[pk, smk], writes=["acc"])
                    if first_head:
                        P.op("dve", TT(imp[:, kvh, 2 * q:2 * q + 2, :], v2[:, :, 65:129], rb2, ALU.mult), reads=[pk, smk], writes=["imp"])
                    else:
                        P.op("dve", TT(etmp[:, 2 * q:2 * q + 2, :], v2[:, :, 65:129], rb2, ALU.mult), reads=[pk, smk], writes=["etmp"])
                        P.op("pool", TT(imp[:, kvh, 2 * q:2 * q + 2, :], imp[:, kvh, 2 * q:2 * q + 2, :], etmp[:, 2 * q:2 * q + 2, :], ALU.add),
                             reads=["etmp", "imp"], writes=["imp"])
            else:
                bv = pb[0][0][:, :].rearrange("p (j c) -> p j c", c=128)
                fb = S[:, 8:12].unsqueeze(2).to_broadcast([128, 4, 64])
                P.op("dve", TT(etmp[:], bv[:, :, 0:64], fb, ALU.mult), reads=pks + [smk], writes=["etmp"])
                P.op("pool", TT(acc[:, :, hcols], acc[:, :, hcols], etmp[:], ALU.add), reads=["etmp", "acc"], writes=["acc"])

        for n_, sc in enumerate(chunks):
            a_, b_ = make(sc, n_ == len(chunks) - 1)
            push(a_, b_)

    for tt in range(n_att):
        ts = slice(tt * 512, (tt + 1) * 512)
        ncs = [0, 1] if tt >= 4 else [0]
        for h in range(8):
            attend(tt, h, 0, KCCd[h // 4], f"KCCd{h // 4}", lambda sc, kvh: VCa[:, sc, kvh, 0:129], "VCa", ncs,
                   lambda sc: D0[:, tt * 512 - 2048 * sc:tt * 512 - 2048 * sc + 512], "D0",
                   lambda j: (0, ncs[-1]), 129, first_head=(h % 4 == 0))
        flush()
        P.phase(f'att{tt}_topk')
        for kvh in range(2):
            for j in range(4):
                sub = 4 * tt + j
                o_ = 62 - 2 * sub
                P.op("dve", TT(iw[:], imp[:, kvh, j, :], kbab[:, o_:o_ + 64], ALU.mult), reads=["imp", "kbab"], writes=["iw"])
                P.op("dve", TT(iw[:], iw[:], kbab[:, 128 + o_:128 + o_ + 64], ALU.add), reads=["iw", "kbab"], writes=["iw"])
                P.op("dve", MS(iw[:, 0:1], 1.0e6), reads=["iw"], writes=["iw"])
                P.op("dve", lambda e: e.max(out=m8[:, 0:8], in_=iw[:]), reads=["iw"], writes=["m8"])
                P.op("dve", lambda e: e.match_replace(out=iw2[:], in_to_replace=m8[:, 0:8], in_values=iw[:], imm_value=-3.0e38),
                     reads=["iw", "m8"], writes=["iw2"])
                P.op("dve", lambda e: e.max(out=m8[:, 8:16], in_=iw2[:]), reads=["iw2"], writes=["m8"])
                for dup in range(2):
                    P.op("dve", TS(selb2[:, dup * 64:(dup + 1) * 64], iw[:], m8[:, 15:16], ALU.is_lt, -30000.0, ALU.mult),
                         reads=["iw", "m8"], writes=["selb2"])
                P.op("pe", TR(B6[:, j * 128:(j + 1) * 128], selb2[:], identb[:]), reads=["selb2", "identb"], writes=["B6"])
            P.op("act", ACT(SELBT[kvh][:], B6[:, 0:512], AF.Copy), reads=["B6"], writes=[f"SELBT{kvh}"])
        P.phase(f'att{tt}_sel')
        for h in range(8):
            attend(tt, h, 1, KSd[h // 4], f"KSd{h // 4}", lambda sc, kvh: VSa[:, sc, kvh, :], "VSa", list(range(0, 4 * tt + 4)),
                   lambda sc: Tz[:, 512 * tt - 128 * sc + 384:512 * tt - 128 * sc + 384 + 512], "Tz",
                   lambda j: (0, 4 * tt + j), 65, sel=True)
        P.phase(f'att{tt}_win')
        for h in range(8):
            attend(tt, h, 2, KWd[h // 4], f"KWd{h // 4}", lambda sc, kvh: VWa[:, sc, kvh, :], "VWa", list(range(max(0, 4 * tt - 4), 4 * tt + 4)),
                   lambda sc: TzW[:, 512 * tt - 128 * sc + 384:512 * tt - 128 * sc + 384 + 512], "TzW",
                   lambda j: (max(0, 4 * tt + j - 4), 4 * tt + j), 65)
        flush()
        P.op("act", ACT(accb[:], acc[:], AF.Copy), reads=["acc"], writes=["accb"])
        for c in range(4):
            for j in range(4):
                P.op("pe", TR(B6[:, j * 128:(j + 1) * 128], accb[:, j, c * 128:(c + 1) * 128], identb[:]), reads=["accb", "identb"], writes=["B6"])
            i = mxc[0] % 2
            mxc[0] += 1
            P.op("dve", CP(MXo[i][:], B6[:, 0:512]), reads=["B6"], writes=[f"MXo{i}"])
            P.dma("sp", mixT[c, :, ts], MXo[i][:], reads=[f"MXo{i}"], writes=[f"mixT{c}_{tt}"])
    P.barrier()
    P.release()
    if stage == "nsa":
        return P
    P.phase('ffn_w')
    w_outd = nc.dram_tensor("w_out", [8, 128, 1024], F32, kind="ExternalInput").ap()
    w_ff1d = nc.dram_tensor("w_ff1", [8, 128, 4096], F32, kind="ExternalInput").ap()
    w_ff2d = nc.dram_tensor("w_ff2", [32, 128, 1024], F32, kind="ExternalInput").ap()
    outT = nc.dram_tensor("outT", [8, 128, T], F32, kind="ExternalOutput").ap()
    P.mark()
    Wo = P.sb([128, 8, 1024], BF16)
    W1f = P.sb([128, 8, 4096], BF16)
    W2f = P.sb([128, 32, 1024], BF16)
    Hh = P.sb([128, 32, 256], BF16)
    _hb = P.sb_off - 16384
    fst = [P.sb_at([128, 2048], F32, _hb + i_ * 8192) for i_ in range(2)]
    fctr = [0]
    cast_eng = ["pool", "dve", "act"]

    def load_cast(dst, src, view=None, wkey="Wf"):
        i = fctr[0] % 2
        e = cast_eng[fctr[0] % 3]
        fctr[0] += 1
        sv = fst[i][:, :] if view is None else view(fst[i])
        P.dma("sp", sv, src, writes=[f"fst{i}"])
        if e == "act":
            P.op("act", ACT(dst, sv, AF.Copy), reads=[f"fst{i}"], writes=[wkey])
        else:
            P.op(e, CP(dst, sv), reads=[f"fst{i}"], writes=[wkey])

    for dc in range(4):
        load_cast(Wo[:, 2 * dc:2 * dc + 2, :], w_outd[2 * dc:2 * dc + 2].rearrange("a p b -> p a b"),
                  view=lambda t_: t_[:, :].rearrange("p (a b) -> p a b", a=2), wkey="Wo")
    for dc in range(8):
        for hf in range(2):
            load_cast(W1f[:, dc, hf * 2048:(hf + 1) * 2048], w_ff1d[dc, :, hf * 2048:(hf + 1) * 2048], wkey="W1f")
    for f2 in range(16):
        load_cast(W2f[:, 2 * f2:2 * f2 + 2, :], w_ff2d[2 * f2:2 * f2 + 2].rearrange("a p b -> p a b"),
                  view=lambda t_: t_[:, :].rearrange("p (a b) -> p a b", a=2), wkey="W2f")
    P.phase('ffn')
    TW = 256
    xin3 = [P.sb([128, 8, TW], F32) for _ in range(2)]
    MX = [P.sb([128, 8, TW], BF16) for _ in range(2)]
    sq3 = P.sb([128, 8, TW], BF16)
    sd3 = P.sb([128, TW], F32)
    xn1 = P.sb([128, 8, TW], BF16)
    rl = [P.sb([128, TW], F32) for _ in range(2)]
    ot = [P.sb([128, TW], F32) for _ in range(2)]
    bctr = [0]

    def nb():
        i = bctr[0] % 6
        bctr[0] += 1
        return B[i], f"B{i}"

    def ffn_load(t3):
        ts = slice(t3 * TW, (t3 + 1) * TW)
        p = t3 % 2
        P.dma("sp", xin3[p][:], xT[:, :, ts].rearrange("c p t -> p c t"), writes=[f"xin3{p}"])
        P.dma("sp", MX[p][:], mixT[:, :, ts].rearrange("c p t -> p c t"), reads=[f"mixT{c}_{t3 * TW // 512}" for c in range(8)], writes=[f"MX{p}"])

    ffn_load(0)
    for t3 in range(T // TW):
        ts = slice(t3 * TW, (t3 + 1) * TW)
        p = t3 % 2
        xk, mk = f"xin3{p}", f"MX{p}"
        if t3 + 1 < T // TW:
            ffn_load(t3 + 1)
        for dch in range(8):
            bank, bk = nb()
            for c in range(8):
                P.op("pe", MM(bank[:, 0:TW], Wo[:, c, dch * 128:(dch + 1) * 128], MX[p][:, c, :], start=c == 0, stop=c == 7), reads=["Wo", mk], writes=[bk])
            P.op("dve", TT(xin3[p][:, dch, :], bank[:, 0:TW], xin3[p][:, dch, :], ALU.add), reads=[bk, xk], writes=[xk])
        P.op("act", ACT(sq3[:], xin3[p][:], AF.Square), reads=[xk], writes=["sq3"])
        for dc in range(8):
            P.op("pe", MM(B7[:, 0:TW], onesb[:], sq3[:, dc, :], start=dc == 0, stop=dc == 7), reads=["sq3", "onesb"], writes=["B7"])
        P.op("act", ACT(sd3[:], B7[:, 0:TW], AF.Ln, scale=1.0 / 1024, bias=1e-6), reads=["B7"], writes=["sd3"])
        P.op("act", ACT(sd3[:], sd3[:], AF.Exp, scale=-0.5), reads=["sd3"], writes=["sd3"])
        for dc in range(8):
            P.op("dve", STT(xn1[:, dc, :], xin3[p][:, dc, :], pv[:, PV_G2 + dc:PV_G2 + dc + 1], sd3[:], ALU.mult, ALU.mult),
                 reads=[xk, "pv", "sd3"], writes=["xn1"])
        for f in range(32):
            bank, bk = nb()
            for dc in range(8):
                P.op("pe", MM(bank[:, 0:TW], W1f[:, dc, f * 128:(f + 1) * 128], xn1[:, dc, :], start=dc == 0, stop=dc == 7), reads=["W1f", "xn1"], writes=[bk])
            r = rl[f % 2]
            P.op("act", ACT(r[:], bank[:, 0:TW], AF.Relu), reads=[bk], writes=[f"rl{f % 2}"])
            P.op("pool", TT(Hh[:, f, :], r[:], r[:], ALU.mult), reads=[f"rl{f % 2}"], writes=["Hh", "fst0", "fst1"])
        for dch in range(8):
            bank, bk = nb()
            for f in range(32):
                P.op("pe", MM(bank[:, 0:TW], W2f[:, f, dch * 128:(dch + 1) * 128], Hh[:, f, :], start=f == 0, stop=f == 31), reads=["W2f", "Hh"], writes=[bk])
            o = ot[dch % 2]
            P.op("dve", TT(o[:], bank[:, 0:TW], xin3[p][:, dch, :], ALU.add), reads=[bk, xk], writes=[f"ot{dch % 2}"])
            P.dma("sp", outT[dch, :, ts], o[:], reads=[f"ot{dch % 2}"])
    P.barrier()
    P.release()
    return P


def host_inputs(inputs, b):
    f = lambda a: np.ascontiguousarray(a, dtype=np.float32)
    x = inputs["x"][b]
    d = {}
    d["xT"] = f(x.T.reshape(8, 128, T))
    w = np.asarray(inputs["w_in"][0], np.float32)
    wp = np.zeros((1024, WCOLS), np.float32)
    wp[:, 0:512] = w[:, 0:512]
    wp[:, 512:640] = w[:, 512:640]
    wp[:, 640:768] = w[:, 640:768]
    for kv in range(2):
        wp[:, 768 + 128 * kv:768 + 128 * kv + 64] = w[:, 768 + 64 * kv:768 + 64 * kv + 64]
        wp[:, 768 + 128 * kv + 64:768 + 128 * kv + 128] = w[:, 768 + 64 * kv:768 + 64 * kv + 64]
        wp[:, 1152 + 128 * kv:1152 + 128 * kv + 64] = w[:, 1024 + 64 * kv:1024 + 64 * kv + 64]
        wp[:, 1152 + 128 * kv + 64:1152 + 128 * kv + 128] = w[:, 1024 + 64 * kv:1024 + 64 * kv + 64]
    wp[:, 1024:1152] = w[:, 896:1024]
    wp[:, 1408:1536] = w[:, 1152:1280]
    wp[:, RW0:RW0 + 1792] = w[:, 1304:3096]
    d["w_in"] = f(wp.reshape(8, 128, WCOLS))
    d["w_gl"] = f(w[:, 1280:1304].reshape(8, 128, 24).transpose(1, 0, 2))
    pv = np.zeros((128, NPV), np.float32)
    pv[:, PV_G:PV_G + 8] = inputs["ln_mix_g"][0].reshape(8, 128).T
    pv[:, PV_MU:PV_MU + 14] = inputs["rwkv_mu"][0].reshape(14, 128).T
    for nm, c in (("rwkv_w0", PV_W0), ("rwkv_a0", PV_A0), ("rwkv_k_k", PV_KK), ("rwkv_k_a", PV_KA),
                  ("rwkv_lnx_w", PV_LW), ("rwkv_lnx_b", PV_LB)):
        pv[:, c:c + 4] = inputs[nm][0].reshape(4, 128).T
    pv[:, PV_RK:PV_RK + 4] = inputs["rwkv_r_k"][0].reshape(4, 128).T
    d["pvec"] = pv
    d["w2a2"] = f(np.concatenate([inputs["rwkv_w2"][0], inputs["rwkv_a2"][0]], axis=0))
    d["g2"] = f(inputs["rwkv_g2"][0])
    o = np.ones((128, 128), np.float32)
    msl, msu, miu = np.tril(o, -1), np.triu(o, 1), np.triu(o, 0)
    bo = np.kron(np.eye(2, dtype=np.float32), np.ones((64, 64), np.float32))
    d["cst"] = f(np.concatenate([msl, msu, msu, miu, miu, np.eye(128, dtype=np.float32), bo], axis=1))
    t64 = lambda v: np.tile(np.asarray(v, np.float32), 2)
    pv[:, PV_QG] = t64(inputs["q_norm_g"][0])
    pv[:, PV_KSG] = t64(inputs["ks_norm_g"][0])
    pv[:, PV_KWG] = t64(inputs["kw_norm_g"][0])
    pv[:, PV_KCG] = t64(inputs["kc_norm_g"][0])
    pv[:, PV_G2:PV_G2 + 8] = inputs["ln_ffn_g"][0].reshape(8, 128).T
    d["gb"] = f(np.tile(inputs["nsa_gate_b"][0][None, :], (128, 1)))
    for nm, src in (("w1k", "cmp_k_w1"), ("w1v", "cmp_v_w1")):
        w1 = np.asarray(inputs[src][0], np.float32).reshape(32, 64, 128).transpose(1, 0, 2)
        d[nm] = f(np.concatenate([w1, w1], axis=0))
    for nm, src in (("posk", "cmp_k_pos"), ("posv", "cmp_v_pos")):
        pt = np.asarray(inputs[src][0], np.float32).T
        d[nm] = f(np.concatenate([pt, pt], axis=0))
    w2k = np.asarray(inputs["cmp_k_w2"][0], np.float32)
    d["w2k"] = f(np.concatenate([w2k, w2k], axis=1))
    d["w2v"] = f(inputs["cmp_v_w2"][0])
    n_ = np.arange(256)[:, None]
    j_ = np.arange(64)[None, :]
    ovl = ((16 * n_ <= 64 * j_ + 63) & (16 * n_ + 31 >= 64 * j_)).astype(np.float32)
    ovl[255] = 0
    on = np.ones((256, 1), np.float32)
    on[255] = 0
    d["ovla"] = f(np.concatenate([on, ovl], axis=1).reshape(2, 128, 65).transpose(1, 0, 2))
    p_ = np.arange(128)[:, None].astype(np.float64)
    jx = np.arange(4480)[None, :] - 384 - p_
    d["tz"] = f(np.where(jx >= 0, -jx, NEGB))
    jw = np.arange(1408)[None, :] - 384 - p_
    d["tzw"] = f(np.where((jw >= 0) & (jw < 512), -jw, NEGB))
    dd = np.arange(4096)[None, :] - 16 * p_ - 31
    d["d0"] = f(np.where(dd >= 0, -dd, NEGB))
    d["expand"] = f((np.arange(4096)[None, :] // 64 == (np.arange(128)[:, None] % 64)))
    y_ = np.arange(128)[None, :]
    cpos = np.where(np.arange(128)[:, None] < 64, 62, 63)
    kb = np.where(y_ >= cpos - 1, 0.0, 1.0)
    ab = np.where(y_ > cpos, -1.0e30, np.where(y_ >= cpos - 1, 1.0e6, 0.0))
    d["kbab"] = f(np.concatenate([kb, ab], axis=1))
    d["w_out"] = f(np.asarray(inputs["w_out"][0]).reshape(8, 128, 1024))
    d["w_ff1"] = f(np.asarray(inputs["w_ff1"][0]).reshape(8, 128, 4096))
    d["w_ff2"] = f(np.asarray(inputs["w_ff2"][0]).reshape(32, 128, 1024))
    return d


_CACHE = {}


def kernel(**inputs):
    inputs = {k: np.asarray(v) for k, v in inputs.items()}
    if "nc" not in _CACHE:
        _CACHE["nc"] = build().emit()
    nc = _CACHE["nc"]
    maps = [host_inputs(inputs, b) for b in range(8)]
    res = run_bass_kernel_spmd(nc, maps, core_ids=list(range(8)))
    out = np.empty((8, T, 1024), np.float32)
    for b in range(8):
        out[b] = np.asarray(res.results[b]["outT"], np.float32).reshape(1024, T).T
    return out
```
